# Optimizing a Trainium2 kernel written in Bass

```python
import math
import jax, jax.numpy as jnp
from jax import lax
import numpy as np

D_MODEL = 1024
BATCH = 4
SEQ = 8192
DEPTH = 2

HEAD_DIM = 64
D_MIX = 2 * D_MODEL
A_WIDTH = 3 * D_MIX // 8
B_WIDTH = 3 * D_MIX // 8
C_WIDTH = D_MIX // 4
A_Q_HEADS = A_WIDTH // HEAD_DIM
A_KV_HEADS = A_Q_HEADS // 4
A_WINDOW = 128
B_HEADS = B_WIDTH // HEAD_DIM
B_PATTERNS = ((128, 1), (512, 4), (2048, 16))
C_V_DIM = 2 * HEAD_DIM
C_HEADS = C_WIDTH // C_V_DIM
N_BIAS_HEADS = A_Q_HEADS + B_HEADS + C_HEADS
REL_BUCKETS = 32
REL_MAX_DIST = 2048
BLK = 128
EPS = 1e-6
NEG = -1e30
PROJ_SIZES = (A_Q_HEADS * HEAD_DIM, A_KV_HEADS * HEAD_DIM, A_KV_HEADS * HEAD_DIM,
              B_WIDTH, B_WIDTH, B_WIDTH,
              2 * C_HEADS * HEAD_DIM, 2 * C_HEADS * HEAD_DIM, C_HEADS * C_V_DIM,
              D_MIX)
PROJ_OUT = sum(PROJ_SIZES)

kernel_name = "hybrid_swa_dilated_diff_attn_block"


def rms_norm(x, g):
    xf = x.astype(jnp.float32)
    y = xf * lax.rsqrt(jnp.mean(xf * xf, axis=-1, keepdims=True) + EPS)
    return (y * g.astype(jnp.float32)).astype(x.dtype)


def rel_bucket(dist):
    n = jnp.maximum(dist, 0)
    max_exact = REL_BUCKETS // 2
    nf = jnp.maximum(n, 1).astype(jnp.float32)
    large = max_exact + (jnp.log(nf / max_exact) / math.log(REL_MAX_DIST / max_exact)
                         * (REL_BUCKETS - max_exact)).astype(jnp.int32)
    large = jnp.minimum(large, REL_BUCKETS - 1)
    return jnp.where(n < max_exact, n, large)


def banded_attention(q, k, v, table_h, max_dist, dist_scale, sink=None):
    Bn, L, Hq, dh = q.shape
    Hkv = k.shape[2]
    G = Hq // Hkv
    nb = L // BLK
    dist = jnp.arange(BLK)[:, None] + BLK - jnp.arange(2 * BLK)[None, :]
    valid = (jnp.arange(nb)[:, None] * BLK - BLK + jnp.arange(2 * BLK)[None, :]) >= 0
    allowed = ((dist >= 0) & (dist <= max_dist))[None] & valid[:, None, :]
    bias = table_h[rel_bucket(dist * dist_scale)].astype(jnp.float32)
    bias = bias.transpose(2, 0, 1).reshape(Hkv, G, BLK, 2 * BLK)

    def key_blocks(t):
        tp = jnp.pad(t, ((0, 0), (BLK, 0), (0, 0), (0, 0)))
        prev = tp[:, :L].reshape(Bn, nb, BLK, Hkv, dh)
        cur = t.reshape(Bn, nb, BLK, Hkv, dh)
        return jnp.concatenate([prev, cur], axis=2)

    kb, vb = key_blocks(k), key_blocks(v)
    qb = q.reshape(Bn, nb, BLK, Hkv, G, dh)
    s = jnp.einsum('bnqhgd,bnkhd->bnhgqk', qb, kb,
                   preferred_element_type=jnp.float32) * (1.0 / math.sqrt(dh))
    s = s + bias[None, None]
    s = jnp.where(allowed[None, :, None, None], s, NEG)
    m = jnp.max(s, axis=-1)
    if sink is not None:
        sk = sink.astype(jnp.float32).reshape(Hkv, G)[None, None, :, :, None]
        m = jnp.maximum(m, sk)
    e = jnp.exp(s - m[..., None])
    denom = jnp.sum(e, axis=-1)
    if sink is not None:
        denom = denom + jnp.exp(sk - m)
    p = e / denom[..., None]
    lse = m + jnp.log(denom)
    o = jnp.einsum('bnhgqk,bnkhd->bnqhgd', p.astype(v.dtype), vb)
    o = o.reshape(Bn, L, Hq, dh)
    lse = lse.transpose(0, 1, 4, 2, 3).reshape(Bn, L, Hq)
    return o, lse


def to_strided(t, d, Sp):
    Bn, S = t.shape[:2]
    t = jnp.pad(t, [(0, 0), (0, Sp - S)] + [(0, 0)] * (t.ndim - 2))
    t = t.reshape((Bn, Sp // d, d) + t.shape[2:])
    t = jnp.moveaxis(t, 2, 1)
    return t.reshape((Bn * d, Sp // d) + t.shape[3:])


def from_strided(t, d, Bn, S):
    Sp = t.shape[1] * d
    t = t.reshape((Bn, d, Sp // d) + t.shape[2:])
    t = jnp.moveaxis(t, 1, 2).reshape((Bn, Sp) + t.shape[3:])
    return t[:, :S]


def dilated_mixture(q, k, v, table_b):
    Bn, S = q.shape[:2]
    outs, lses = [], []
    for (w, d) in B_PATTERNS:
        span = d * BLK
        Sp = -(-S // span) * span
        o, lse = banded_attention(to_strided(q, d, Sp), to_strided(k, d, Sp), to_strided(v, d, Sp),
                                  table_b, w // d, d)
        outs.append(from_strided(o, d, Bn, S))
        lses.append(from_strided(lse, d, Bn, S))
    alpha = jax.nn.softmax(jnp.stack(lses, axis=0), axis=0)
    o = jnp.einsum('pbsh,pbshd->bshd', alpha, jnp.stack(outs, axis=0).astype(jnp.float32))
    return o.astype(q.dtype)


def diff_attention(q, k, v, table_c, lam):
    Bn, S, H, _, dh = q.shape
    nb = S // BLK
    qb = jnp.moveaxis(q.reshape(Bn, nb, BLK, H, 2, dh), 1, 0)
    kpos = jnp.arange(S)
    scale = 1.0 / math.sqrt(dh)

    def block(args):
        i, qi = args
        dist = (i * BLK + jnp.arange(BLK))[:, None] - kpos[None, :]
        bias = table_c[rel_bucket(dist)].astype(jnp.float32).transpose(2, 0, 1)
        s = jnp.einsum('bqhmd,bkhmd->bhmqk', qi, k, preferred_element_type=jnp.float32) * scale
        s = s + bias[None, :, None]
        s = jnp.where((dist >= 0)[None, None, None], s, NEG)
        p = jax.nn.softmax(s, axis=-1)
        a = p[:, :, 0] - lam * p[:, :, 1]
        return jnp.einsum('bhqk,bkhe->bqhe', a.astype(v.dtype), v)

    o = lax.map(block, (jnp.arange(nb), qb))
    return jnp.moveaxis(o, 0, 1).reshape(Bn, S, H, C_V_DIM)


def hybrid_layer(x, c_act, layer_idx, rel_table, w_in, w_out, w_ada, b_ada, g_pre, g_post,
                 a_sinks, lam_q1, lam_k1, lam_q2, lam_k2, g_sub):
    Bn, S, _ = x.shape
    mod = c_act @ w_ada + b_ada
    shift, scale, gate = jnp.split(mod, 3, axis=-1)
    h = rms_norm(x, g_pre) * (1 + scale[:, None]) + shift[:, None]
    proj = h @ w_in
    cuts, acc = [], 0
    for sz in PROJ_SIZES[:-1]:
        acc += sz
        cuts.append(acc)
    aq, ak, av, bq, bk, bv, cq, ck, cv, z = jnp.split(proj, cuts, axis=-1)

    ya, _ = banded_attention(aq.reshape(Bn, S, A_Q_HEADS, HEAD_DIM),
                             ak.reshape(Bn, S, A_KV_HEADS, HEAD_DIM),
                             av.reshape(Bn, S, A_KV_HEADS, HEAD_DIM),
                             rel_table[:, :A_Q_HEADS], A_WINDOW - 1, 1, sink=a_sinks)
    ya = ya.reshape(Bn, S, A_WIDTH)

    yb = dilated_mixture(bq.reshape(Bn, S, B_HEADS, HEAD_DIM), bk.reshape(Bn, S, B_HEADS, HEAD_DIM),
                         bv.reshape(Bn, S, B_HEADS, HEAD_DIM),
                         rel_table[:, A_Q_HEADS:A_Q_HEADS + B_HEADS])
    yb = yb.reshape(Bn, S, B_WIDTH)

    lam_init = 0.8 - 0.6 * math.exp(-0.3 * layer_idx)
    lam = (jnp.exp(jnp.sum(lam_q1.astype(jnp.float32) * lam_k1.astype(jnp.float32)))
           - jnp.exp(jnp.sum(lam_q2.astype(jnp.float32) * lam_k2.astype(jnp.float32))) + lam_init)
    yc = diff_attention(cq.reshape(Bn, S, C_HEADS, 2, HEAD_DIM), ck.reshape(Bn, S, C_HEADS, 2, HEAD_DIM),
                        cv.reshape(Bn, S, C_HEADS, C_V_DIM), rel_table[:, A_Q_HEADS + B_HEADS:], lam)
    yc = (rms_norm(yc, g_sub) * (1.0 - lam_init)).reshape(Bn, S, C_WIDTH)

    y = jnp.concatenate([ya, yb, yc], axis=-1) * jax.nn.silu(z)
    y = y @ w_out
    return x + gate[:, None] * rms_norm(y, g_post)


def setup_inputs(seed: int = 0) -> dict:
    key = jax.random.key(seed)
    ks = jax.random.split(key, 16)
    f32 = jnp.float32
    nrm = lambda k, shape, s: jax.random.normal(k, shape, f32) * s
    return {
        'x': nrm(ks[0], (BATCH, SEQ, D_MODEL), 1.0),
        'c': nrm(ks[1], (BATCH, D_MODEL), 1.0),
        'rel_table': nrm(ks[2], (REL_BUCKETS, N_BIAS_HEADS), 0.5),
        'w_in': nrm(ks[3], (DEPTH, D_MODEL, PROJ_OUT), D_MODEL ** -0.5),
        'w_out': nrm(ks[4], (DEPTH, D_MIX, D_MODEL), D_MIX ** -0.5),
        'w_ada': nrm(ks[5], (DEPTH, D_MODEL, 3 * D_MODEL), 0.5 * D_MODEL ** -0.5),
        'b_ada': nrm(ks[6], (DEPTH, 3 * D_MODEL), 0.01),
        'g_pre': 1.0 + nrm(ks[7], (DEPTH, D_MODEL), 0.05),
        'g_post': 1.0 + nrm(ks[8], (DEPTH, D_MODEL), 0.05),
        'a_sinks': nrm(ks[9], (DEPTH, A_Q_HEADS), 0.5),
        'lam_q1': nrm(ks[10], (DEPTH, HEAD_DIM), 0.1),
        'lam_k1': nrm(ks[11], (DEPTH, HEAD_DIM), 0.1),
        'lam_q2': nrm(ks[12], (DEPTH, HEAD_DIM), 0.1),
        'lam_k2': nrm(ks[13], (DEPTH, HEAD_DIM), 0.1),
        'g_sub': 1.0 + nrm(ks[14], (DEPTH, C_V_DIM), 0.05),
    }


def reference(x, c, rel_table, w_in, w_out, w_ada, b_ada, g_pre, g_post, a_sinks,
              lam_q1, lam_k1, lam_q2, lam_k2, g_sub):
    c_act = jax.nn.silu(c)
    for l in range(DEPTH):
        x = hybrid_layer(x, c_act, l, rel_table, w_in[l], w_out[l], w_ada[l], b_ada[l], g_pre[l],
                         g_post[l], a_sinks[l], lam_q1[l], lam_k1[l], lam_q2[l], lam_k2[l], g_sub[l])
    return x
```

```python
import math
import contextlib
import numpy as np
import concourse.bass as bass
import concourse.mybir as mybir
from concourse.bass_utils import run_bass_kernel_spmd

F32 = mybir.dt.float32
BF16 = mybir.dt.bfloat16
AF = mybir.ActivationFunctionType
ALU = mybir.AluOpType
AX = mybir.AxisListType

D = 1024
NL = 2
EPS = 1e-6
NQK = 3520
NVZ = 3520
AQ0, AK0, BQ0, BK0, CQ0, CK0 = 0, 768, 960, 1728, 2496, 3008
AV0, BV0, CV0, Z0 = 0, 192, 960, 1472
CSTRIP = 2560
CNEAR = 13
LAM_INIT = [0.8 - 0.6 * math.exp(-0.3 * l) for l in range(NL)]


class Buf:
    __slots__ = ("name", "w", "r", "dsem", "dcnt")

    def __init__(self, name):
        self.name = name
        self.w = []
        self.r = []
        self.dsem = None
        self.dcnt = 0


class Sync:
    def __init__(self, nc, stack):
        self.nc = nc
        self.stack = stack
        self.eng = {"pe": nc.tensor, "act": nc.scalar, "dve": nc.vector, "pool": nc.gpsimd, "sp": nc.sync}
        self.sem = {}
        for k in ("pe", "act", "dve", "pool"):
            self.sem[k] = stack.enter_context(nc.semaphore("e_" + k))
        self.cnt = {k: 0 for k in self.sem}
        self.seen = {k: {} for k in self.eng}
        self.dbufs = []
        self.dcount = {}
        self.nsem = 4

    def _wait(self, e, ev):
        key, val = ev
        if key == "pe" and e == "pe":
            return
        if self.seen[e].get(key, 0) >= val:
            return
        self.eng[e].wait_ge(self.sem[key], val)
        self.seen[e][key] = val

    def _deps(self, e, reads, writes):
        for b in reads:
            for ev in b.w:
                self._wait(e, ev)
        for b in writes:
            for ev in b.w:
                self._wait(e, ev)
            for ev in b.r:
                self._wait(e, ev)

    def op(self, e, fn, reads=(), writes=()):
        self._deps(e, reads, writes)
        ins = fn(self.eng[e])
        self.cnt[e] += 1
        ins.then_inc(self.sem[e], 1)
        ev = (e, self.cnt[e])
        for b in reads:
            b.r = [x for x in b.r if x[0] != e] + [ev]
        for b in writes:
            b.w = [x for x in b.w if x[0] != e] + [ev]
            b.r = []
        return ins

    def pe_nosig(self, fn, reads=(), writes=()):
        self._deps("pe", reads, writes)
        return fn(self.eng["pe"])

    def _dsem(self, b):
        if b.dsem is None:
            key = "d_" + b.name
            if key not in self.sem:
                self.sem[key] = self.stack.enter_context(self.nc.semaphore(key))
                self.dcount[key] = 0
                self.nsem += 1
            b.dsem = key
            b.dcnt = self.dcount[key]
            self.dbufs.append(b)
        return b.dsem

    def dma(self, q, out_ap, in_ap, buf, load, extra_reads=(), **kw):
        key = self._dsem(buf)
        if load:
            self._deps(q, extra_reads, [buf])
        else:
            self._deps(q, [buf] + list(extra_reads), [])
        ins = self.eng[q].dma_start(out=out_ap, in_=in_ap, **kw)
        buf.dcnt += 16
        self.dcount[key] = buf.dcnt
        ins.then_inc(self.sem[key], 16)
        ev = (key, buf.dcnt)
        if load:
            buf.w = [x for x in buf.w if x[0] != key] + [ev]
            buf.r = []
        else:
            buf.r = [x for x in buf.r if x[0] != key] + [ev]
        return ins

    def barrier(self):
        evs = [(k, self.cnt[k]) for k in ("pe", "act", "dve", "pool") if self.cnt[k] > 0]
        evs += [(k, v) for k, v in self.dcount.items() if v > 0]
        for e in self.eng:
            for ev in evs:
                if ev[0] == e:
                    continue
                self._wait(e, ev)
        for b in self.dbufs:
            b.w = []
            b.r = []
        self.dbufs = []


def build(S=8192, nl=NL, dbg=False, stop=99):
    NT = S // 128
    NST = S // 512
    NSP = S // 2048
    nc = bass.Bass("TRN2", target_bir_lowering=False)

    def din(name, shape):
        return nc.dram_tensor(name, shape, F32, kind="ExternalInput").ap()

    x_in = din("xb", [S, D])
    c_in = din("c2", [128, 8])
    w_in = din("w_in", [NL, D, 7040])
    w_out = din("w_out", [NL, 2048, D])
    w_ada = din("w_ada", [NL, D, 3072])
    b_ada = din("b_ada", [NL, 3072])
    g_pre = din("g_pre", [NL, D])
    g_post = din("g_post", [NL, D])
    a_sinks = din("a_sinks", [NL, 12])
    lam4 = din("lam4", [NL, 256])
    g_sub = din("g_sub", [NL, 128])
    biasAB = din("biasAB", [128, 12, 1024])
    maskAB = din("maskAB", [128, 12, 1024])
    biasC = din("biasC", [128, 4, CSTRIP])
    maskC = din("maskC", [128, CSTRIP])
    b31_in = din("b31", [1, 4])
    ident_in = din("ident", [128, 128])
    out = nc.dram_tensor("out", [S, D], F32, kind="ExternalOutput").ap()

    skind = "ExternalOutput" if dbg else "Internal"

    def scratch(name, shape, dt):
        return nc.dram_tensor(name, shape, dt, kind=skind).ap()

    wbf_in = scratch("wbf_in", [NL, D, 7040], BF16)
    wbf_out = scratch("wbf_out", [NL, 2048, D], BF16)
    modb = scratch("modb", [NL, 3, 128, D], F32)
    ebAB = scratch("ebAB", [128, 12, 1024], BF16)
    ebC = scratch("ebC", [128, 4, CSTRIP], BF16)
    qkT = scratch("qkT", [NQK, S], BF16)
    vz = scratch("vz", [S, NVZ], BF16)
    OB = scratch("OB", [4, S, 780], F32)
    YC = scratch("YC", [S, 512], F32)
    x1 = scratch("x1", [S, D], F32)

    with contextlib.ExitStack() as top:
        sy = Sync(nc, top)

        uid = [0]

        def sb(stack, name, shape, dt):
            uid[0] += 1
            return stack.enter_context(nc.sbuf_tensor("s%d_%s" % (uid[0], name), shape, dt))

        def ps(stack, name, shape, dt):
            uid[0] += 1
            return stack.enter_context(nc.psum_tensor("p%d_%s" % (uid[0], name), shape, dt))

        ident = sb(top, "ident", [128, 128], BF16)
        neglam = sb(top, "neglam", [128, NL], F32)
        esink = sb(top, "esink", [128, NL, 12], F32)
        gsubb = sb(top, "gsubb", [128, NL, 128], F32)
        b31 = sb(top, "b31", [128, 4], F32)
        B_const = Buf("const")

        with contextlib.ExitStack() as ph:
            identf = sb(ph, "identf", [128, 128], F32)
            ones = sb(ph, "ones", [128, 128], F32)
            cin = sb(ph, "cin", [128, 8], F32)
            cact = sb(ph, "cact", [128, 8], F32)
            crep = sb(ph, "crep", [128, 8, 128], F32)
            lamt = sb(ph, "lamt", [128, NL, 256], F32)
            lprod = sb(ph, "lprod", [128, 2, 64], F32)
            lsum = sb(ph, "lsum", [128, 2], F32)
            lexp = sb(ph, "lexp", [128, 2], F32)
            ldiff = sb(ph, "ldiff", [128, 1], F32)
            sinkt = sb(ph, "sinkt", [128, NL, 12], F32)
            gsubt = sb(ph, "gsubt", [128, NL, 128], F32)
            B_identf, B_ones, B_cin, B_cact, B_crep = Buf("identf"), Buf("ones"), Buf("cin"), Buf("cact"), Buf("crep")
            B_lamt, B_lprod, B_lsum, B_lexp, B_ldiff = Buf("lamt"), Buf("lprod"), Buf("lsum"), Buf("lexp"), Buf("ldiff")
            B_sinkt, B_gsubt, B_b31 = Buf("sinkt"), Buf("gsubt"), Buf("b31")

            sy.dma("sp", identf[:], ident_in, B_identf, True)
            sy.op("dve", lambda e: e.tensor_copy(out=ident[:], in_=identf[:]), [B_identf], [B_const])
            sy.op("pool", lambda e: e.memset(ones[:], 1.0), [], [B_ones])
            sy.dma("sp", cin[:], c_in, B_cin, True)
            sy.op("act", lambda e: e.activation(out=cact[:], in_=cin[:], func=AF.Silu), [B_cin], [B_cact])
            for k in range(8):
                sy.op("dve", lambda e, k=k: e.tensor_scalar(out=crep[:, k, :], in0=ones[:], scalar1=cact[:, k:k + 1],
                                                            scalar2=None, op0=ALU.mult), [B_ones, B_cact], [B_crep])
            sy.dma("sp", b31[:], b31_in[0].partition_broadcast(128), B_b31, True)
            sy.dma("sp", lamt[:].rearrange("p l f -> p (l f)"),
                   lam4.rearrange("l f -> (l f)").partition_broadcast(128), B_lamt, True)
            sy.dma("sp", sinkt[:].rearrange("p l f -> p (l f)"),
                   a_sinks.rearrange("l f -> (l f)").partition_broadcast(128), B_sinkt, True)
            sy.dma("sp", gsubt[:].rearrange("p l f -> p (l f)"),
                   g_sub.rearrange("l f -> (l f)").partition_broadcast(128), B_gsubt, True)
            sy.op("act", lambda e: e.activation(out=esink[:].rearrange("p l f -> p (l f)"),
                                                in_=sinkt[:].rearrange("p l f -> p (l f)"), func=AF.Exp),
                  [B_sinkt], [B_const])
            for l in range(NL):
                lt = lamt[:, l, :].rearrange("p (a f) -> p a f", a=4)
                sy.op("dve", lambda e, lt=lt: e.tensor_tensor(out=lprod[:, 0, :], in0=lt[:, 0, :], in1=lt[:, 1, :], op=ALU.mult),
                      [B_lamt], [B_lprod])
                sy.op("dve", lambda e, lt=lt: e.tensor_tensor(out=lprod[:, 1, :], in0=lt[:, 2, :], in1=lt[:, 3, :], op=ALU.mult),
                      [B_lamt], [B_lprod])
                sy.op("dve", lambda e: e.reduce_sum(out=lsum[:], in_=lprod[:], axis=AX.X), [B_lprod], [B_lsum])
                sy.op("act", lambda e: e.activation(out=lexp[:], in_=lsum[:], func=AF.Exp), [B_lsum], [B_lexp])
                sy.op("dve", lambda e: e.tensor_tensor(out=ldiff[:], in0=lexp[:, 1:2], in1=lexp[:, 0:1], op=ALU.subtract),
                      [B_lexp], [B_ldiff])
                sy.op("dve", lambda e, l=l: e.tensor_scalar(out=neglam[:, l:l + 1], in0=ldiff[:], scalar1=-LAM_INIT[l],
                                                            scalar2=None, op0=ALU.add), [B_ldiff], [B_const])
                sy.op("dve", lambda e, l=l: e.tensor_scalar(out=gsubb[:, l, :], in0=gsubt[:, l, :], scalar1=1.0 - LAM_INIT[l],
                                                            scalar2=None, op0=ALU.mult), [B_gsubt], [B_const])

            wt = [sb(ph, "wt%d" % i, [128, 3520], F32) for i in range(2)]
            wb = [sb(ph, "wb%d" % i, [128, 3520], BF16) for i in range(2)]
            B_wt = [Buf("wt%d" % i) for i in range(2)]
            B_wb = [Buf("wb%d" % i) for i in range(2)]
            jobs = []
            for l in range(nl):
                for k in range(8):
                    for h in range(2):
                        jobs.append((w_in[l, k * 128:(k + 1) * 128, h * 3520:(h + 1) * 3520],
                                     wbf_in[l, k * 128:(k + 1) * 128, h * 3520:(h + 1) * 3520], 3520))
                for k in range(16):
                    jobs.append((w_out[l, k * 128:(k + 1) * 128, :], wbf_out[l, k * 128:(k + 1) * 128, :], 1024))
            cv_eng = ["pool", "dve", "act"]
            for i, (src, dst, n) in enumerate(jobs):
                s = i % 2
                sy.dma("sp", wt[s][:, :n], src, B_wt[s], True)
                ce = cv_eng[i % 3]
                if ce == "act":
                    sy.op("act", lambda e, s=s, n=n: e.copy(out=wb[s][:, :n], in_=wt[s][:, :n]), [B_wt[s]], [B_wb[s]])
                else:
                    sy.op(ce, lambda e, s=s, n=n: e.tensor_copy(out=wb[s][:, :n], in_=wt[s][:, :n]), [B_wt[s]], [B_wb[s]])
                sy.dma("pool", dst, wb[s][:, :n], B_wb[s], False)

            wa = [sb(ph, "wa%d" % i, [128, 8, 512], F32) for i in range(2)]
            B_wa = [Buf("wa%d" % i) for i in range(2)]
            brow = sb(ph, "brow", [1, NL * 3072], F32)
            B_brow = Buf("brow")
            modsb = sb(ph, "modsb", [128, 3072], F32)
            B_modsb = Buf("modsb")
            gpb = sb(ph, "gpb", [128, D], F32)
            gqb = sb(ph, "gqb", [128, D], F32)
            a1 = sb(ph, "a1", [128, D], F32)
            gt = sb(ph, "gt", [128, D], F32)
            B_gpb, B_gqb, B_a1, B_gt = Buf("gpb"), Buf("gqb"), Buf("a1"), Buf("gt")
            pm = [ps(ph, "pm%d" % i, [128, 512], F32) for i in range(2)]
            B_pm = [Buf("pm%d" % i) for i in range(2)]
            sy.dma("sp", brow[:], b_ada.rearrange("l f -> (l f)")[None, :], B_brow, True)
            it = 0
            for l in range(nl):
                sy.dma("sp", gpb[:], g_pre[l].partition_broadcast(128), B_gpb, True)
                sy.dma("sp", gqb[:], g_post[l].partition_broadcast(128), B_gqb, True)
                for n in range(6):
                    s = it % 2
                    it += 1
                    sy.dma("sp", wa[s][:], w_ada[l, :, n * 512:(n + 1) * 512].rearrange("(k p) f -> p k f", p=128), B_wa[s], True)
                    for k in range(8):
                        sy.pe_nosig(lambda e, k=k, s=s: e.matmul(pm[s][:], lhsT=crep[:, k, :], rhs=wa[s][:, k, :],
                                                                 start=(k == 0), stop=False), [B_crep, B_wa[s]], [B_pm[s]])
                    sy.op("pe", lambda e, s=s, l=l, n=n: e.matmul(pm[s][:], lhsT=ones[0:1, :],
                                                                  rhs=brow[0:1, l * 3072 + n * 512: l * 3072 + (n + 1) * 512],
                                                                  start=False, stop=True),
                          [B_crep, B_wa[s], B_ones, B_brow], [B_pm[s]])
                    sy.op("dve", lambda e, s=s, n=n: e.tensor_copy(out=modsb[:, n * 512:(n + 1) * 512], in_=pm[s][:]),
                          [B_pm[s]], [B_modsb])
                sy.op("dve", lambda e: e.scalar_tensor_tensor(out=a1[:], in0=modsb[:, 1024:2048], scalar=1.0, in1=gpb[:],
                                                              op0=ALU.add, op1=ALU.mult), [B_modsb, B_gpb], [B_a1])
                sy.op("dve", lambda e: e.tensor_tensor(out=gt[:], in0=modsb[:, 2048:3072], in1=gqb[:], op=ALU.mult),
                      [B_modsb, B_gqb], [B_gt])
                sy.dma("pool", modb[l, 0], a1[:], B_a1, False)
                sy.dma("pool", modb[l, 1], modsb[:, 0:1024], B_modsb, False)
                sy.dma("pool", modb[l, 2], gt[:], B_gt, False)

            bt = [sb(ph, "bt%d" % i, [128, CSTRIP], F32) for i in range(2)]
            mt = [sb(ph, "mt%d" % i, [128, CSTRIP], F32) for i in range(2)]
            eb = [sb(ph, "eb%d" % i, [128, CSTRIP], BF16) for i in range(2)]
            B_bt = [Buf("bt%d" % i) for i in range(2)]
            B_mt = [Buf("mt%d" % i) for i in range(2)]
            B_eb = [Buf("eb%d" % i) for i in range(2)]
            tj = [(biasAB[:, g, :], maskAB[:, g, :], ebAB[:, g, :], 1024) for g in range(12)]
            tj += [(biasC[:, h, :], maskC, ebC[:, h, :], CSTRIP) for h in range(4)]
            negb31 = sb(ph, "negb31", [128, 4], F32)
            B_negb31 = Buf("negb31")
            sy.op("dve", lambda e: e.tensor_scalar(out=negb31[:], in0=b31[:], scalar1=-1.0, scalar2=None, op0=ALU.mult),
                  [B_b31], [B_negb31])
            for i, (bsrc, msrc, dst, n) in enumerate(tj):
                s = i % 2
                sy.dma("sp", bt[s][:, :n], bsrc, B_bt[s], True)
                sy.dma("sp", mt[s][:, :n], msrc, B_mt[s], True)
                if i < 12:
                    sy.op("act", lambda e, s=s, n=n: e.activation(out=bt[s][:, :n], in_=bt[s][:, :n], func=AF.Exp), [B_bt[s]], [B_bt[s]])
                else:
                    hcc = i - 12
                    sy.op("act", lambda e, s=s, n=n, hcc=hcc: e.activation(out=bt[s][:, :n], in_=bt[s][:, :n], func=AF.Exp,
                                                                          bias=negb31[:, hcc:hcc + 1]), [B_bt[s], B_negb31], [B_bt[s]])
                sy.op("dve", lambda e, s=s, n=n: e.tensor_tensor(out=eb[s][:, :n], in0=bt[s][:, :n], in1=mt[s][:, :n], op=ALU.mult),
                      [B_bt[s], B_mt[s]], [B_eb[s]])
                sy.dma("pool", dst, eb[s][:, :n], B_eb[s], False)
            sy.barrier()

        for l in range(nl if stop >= 1 else 0):
            x_src = x_in if l == 0 else x1
            x_dst = out if l == nl - 1 else x1

            with contextlib.ExitStack() as ph:
                W = sb(ph, "W", [128, 8, 7040], BF16)
                B_W = Buf("W")
                a1 = sb(ph, "p1a1", [128, D], F32)
                sh = sb(ph, "p1sh", [128, D], F32)
                B_a1, B_sh = Buf("p1a1"), Buf("p1sh")
                NXB = 3
                xt = [sb(ph, "xt%d" % i, [128, D], F32) for i in range(NXB)]
                B_xt = [Buf("xt%d" % i) for i in range(NXB)]
                hb = [sb(ph, "hb%d" % i, [128, D], BF16) for i in range(2)]
                B_hb = [Buf("hb%d" % i) for i in range(2)]
                hT = [sb(ph, "hT%d" % i, [128, 8, 512], BF16) for i in range(2)]
                B_hT = [Buf("hT%d" % i) for i in range(2)]
                stgT = [sb(ph, "stgT%d" % i, [128, 4, 512], BF16) for i in range(2)]
                B_stgT = [Buf("stgT%d" % i) for i in range(2)]
                stgV = [sb(ph, "stgV%d" % i, [128, NVZ], BF16) for i in range(2)]
                B_stgV = [Buf("stgV%d" % i) for i in range(2)]
                junk = sb(ph, "junk", [128, D], BF16)
                B_junk = Buf("junk")
                st4 = [sb(ph, "st4_%d" % i, [128, 4], F32) for i in range(2)]
                B_st4 = [Buf("st4_%d" % i) for i in range(2)]
                pT = [ps(ph, "pT%d" % i, [128, 8, 128], BF16) for i in range(2)]
                B_pT = [Buf("pT%d" % i) for i in range(2)]
                pm = [ps(ph, "pmm%d" % i, [128, 512], F32) for i in range(4)]
                B_pm = [Buf("pmm%d" % i) for i in range(4)]

                for k in range(8):
                    sy.dma("sp", W[:, k, :], wbf_in[l, k * 128:(k + 1) * 128, :], B_W, True)
                sy.dma("sp", a1[:], modb[l, 0], B_a1, True)
                sy.dma("sp", sh[:], modb[l, 1], B_sh, True)

                def load_x(tt):
                    sy.dma("sp", xt[tt % NXB][:], x_src[tt * 128:(tt + 1) * 128, :], B_xt[tt % NXB], True)

                load_x(0)
                load_x(1)
                pmi = 0
                evi = 0
                for st in range(NST):
                    hs = st % 2
                    for j in range(4):
                        tt = st * 4 + j
                        if tt + 2 < NT:
                            load_x(tt + 2)
                        xs, hbs, ss = tt % NXB, tt % 2, tt % 2
                        X, BX = xt[xs], B_xt[xs]
                        s4, B4 = st4[ss], B_st4[ss]
                        sy.op("act", lambda e, X=X, s4=s4: e.activation(out=junk[:], in_=X[:], func=AF.Square, accum_out=s4[:, 0:1]),
                              [BX], [B_junk, B4])
                        sy.op("dve", lambda e, s4=s4: e.tensor_scalar(out=s4[:, 1:2], in0=s4[:, 0:1], scalar1=1.0 / D, scalar2=EPS,
                                                                      op0=ALU.mult, op1=ALU.add), [B4], [B4])
                        sy.op("act", lambda e, s4=s4: e.sqrt(out=s4[:, 2:3], in_=s4[:, 1:2]), [B4], [B4])
                        sy.op("dve", lambda e, s4=s4: e.reciprocal(out=s4[:, 3:4], in_=s4[:, 2:3]), [B4], [B4])
                        sy.op("dve", lambda e, X=X, s4=s4: e.scalar_tensor_tensor(out=X[:], in0=X[:], scalar=s4[:, 3:4], in1=a1[:],
                                                                                  op0=ALU.mult, op1=ALU.mult), [BX, B4, B_a1], [BX])
                        sy.op("pool", lambda e, X=X, hbs=hbs: e.tensor_tensor(out=hb[hbs][:], in0=X[:], in1=sh[:], op=ALU.add),
                              [BX, B_sh], [B_hb[hbs]])
                        for k in range(8):
                            f = lambda e, k=k, hbs=hbs: e.transpose(out=pT[hbs][:, k, :], in_=hb[hbs][:, k * 128:(k + 1) * 128],
                                                                    identity=ident[:])
                            if k < 7:
                                sy.pe_nosig(f, [B_hb[hbs], B_const], [B_pT[hbs]])
                            else:
                                sy.op("pe", f, [B_hb[hbs], B_const], [B_pT[hbs]])
                        ee = "dve" if (tt % 2 == 0) else "act"
                        if ee == "dve":
                            sy.op("dve", lambda e, hbs=hbs, hs=hs, j=j: e.tensor_copy(out=hT[hs][:, :, j * 128:(j + 1) * 128], in_=pT[hbs][:]),
                                  [B_pT[hbs]], [B_hT[hs]])
                        else:
                            sy.op("act", lambda e, hbs=hbs, hs=hs, j=j: e.copy(out=hT[hs][:, :, j * 128:(j + 1) * 128], in_=pT[hbs][:]),
                                  [B_pT[hbs]], [B_hT[hs]])
                    for c4 in range(7):
                        g = (st * 7 + c4) % 2
                        nrows_tot = 0
                        for i in range(4):
                            c = c4 * 4 + i
                            if c * 128 >= NQK:
                                break
                            ncol = min(128, NQK - c * 128)
                            nrows_tot += ncol
                            p = pmi % 4
                            pmi += 1
                            for k in range(8):
                                f = lambda e, k=k, c=c, ncol=ncol, p=p: e.matmul(pm[p][:ncol, :], lhsT=W[:, k, c * 128:c * 128 + ncol],
                                                                                 rhs=hT[hs][:, k, :], start=(k == 0), stop=(k == 7))
                                if k < 7:
                                    sy.pe_nosig(f, [B_W, B_hT[hs]], [B_pm[p]])
                                else:
                                    sy.op("pe", f, [B_W, B_hT[hs]], [B_pm[p]])
                            evi += 1
                            if evi % 2 == 0:
                                sy.op("dve", lambda e, g=g, i=i, ncol=ncol, p=p: e.tensor_copy(out=stgT[g][:ncol, i, :], in_=pm[p][:ncol, :]),
                                      [B_pm[p]], [B_stgT[g]])
                            else:
                                sy.op("act", lambda e, g=g, i=i, ncol=ncol, p=p: e.copy(out=stgT[g][:ncol, i, :], in_=pm[p][:ncol, :]),
                                      [B_pm[p]], [B_stgT[g]])
                        r0 = c4 * 512
                        nfull = nrows_tot // 128
                        if nfull > 0:
                            sy.dma("sp", qkT[r0:r0 + nfull * 128, st * 512:(st + 1) * 512].rearrange("(i p) t -> p i t", p=128),
                                   stgT[g][:, 0:nfull, :], B_stgT[g], False)
                        rem = nrows_tot - nfull * 128
                        if rem > 0:
                            sy.dma("sp", qkT[r0 + nfull * 128:r0 + nfull * 128 + rem, st * 512:(st + 1) * 512],
                                   stgT[g][:rem, nfull, :], B_stgT[g], False)
                    for j in range(4):
                        tt = st * 4 + j
                        g = tt % 2
                        for n in range(7):
                            ncol = min(512, NVZ - n * 512)
                            p = pmi % 4
                            pmi += 1
                            for k in range(8):
                                f = lambda e, k=k, n=n, ncol=ncol, p=p, j=j: e.matmul(
                                    pm[p][:, :ncol], lhsT=hT[hs][:, k, j * 128:(j + 1) * 128],
                                    rhs=W[:, k, NQK + n * 512:NQK + n * 512 + ncol], start=(k == 0), stop=(k == 7))
                                if k < 7:
                                    sy.pe_nosig(f, [B_W, B_hT[hs]], [B_pm[p]])
                                else:
                                    sy.op("pe", f, [B_W, B_hT[hs]], [B_pm[p]])
                            evi += 1
                            if evi % 2 == 0:
                                sy.op("dve", lambda e, g=g, n=n, ncol=ncol, p=p: e.tensor_copy(out=stgV[g][:, n * 512:n * 512 + ncol], in_=pm[p][:, :ncol]),
                                      [B_pm[p]], [B_stgV[g]])
                            else:
                                sy.op("act", lambda e, g=g, n=n, ncol=ncol, p=p: e.copy(out=stgV[g][:, n * 512:n * 512 + ncol], in_=pm[p][:, :ncol]),
                                      [B_pm[p]], [B_stgV[g]])
                        sy.dma("sp", vz[tt * 128:(tt + 1) * 128, :], stgV[g][:], B_stgV[g], False)
                sy.barrier()
            if stop < 2:
                continue

            with contextlib.ExitStack() as ph:
                EB = sb(ph, "EB", [128, 12, 1024], BF16)
                B_EB = Buf("EB")
                aqT = sb(ph, "aqT", [128, 6, 2048], BF16)
                akT = sb(ph, "akT", [128, 3, 2176], BF16)
                bqT = sb(ph, "bqT", [128, 6, 2048], BF16)
                bkT = sb(ph, "bkT", [128, 6, 2, 2048], BF16)
                B_aqT, B_akT, B_bqT = Buf("aqT"), Buf("akT"), Buf("bqT")
                B_bkT = [Buf("bkT0"), Buf("bkT1")]
                NV = 4
                vaug = [sb(ph, "vaug%d" % i, [128, 12, 65], BF16) for i in range(NV)]
                B_vaug = [Buf("vaug%d" % i) for i in range(NV)]
                Et = [sb(ph, "Et%d" % i, [128, 1024], BF16) for i in range(2)]
                B_Et = [Buf("Et%d" % i) for i in range(2)]
                PTt = [sb(ph, "PTt%d" % i, [128, 1024], BF16) for i in range(3)]
                B_PTt = [Buf("PTt%d" % i) for i in range(3)]
                ostg = [sb(ph, "ostg%d" % i, [128, 780], F32) for i in range(2)]
                B_ostg = [Buf("ostg%d" % i) for i in range(2)]
                pS = [ps(ph, "pS%d" % i, [128, 1024], F32) for i in range(2)]
                B_pS = [Buf("pS%d" % i) for i in range(2)]
                pacc = [ps(ph, "pacc%d" % i, [128, 2, 512], F32) for i in range(2)]
                B_pacc = [Buf("pacc%d" % i) for i in range(2)]

                sy.dma("sp", EB[:], ebAB, B_EB, True)
                for i in range(NV):
                    sy.op("pool", lambda e, i=i: e.memset(vaug[i][:], 1.0), [], [B_vaug[i]])

                cnt = {"v": 0, "s": 0, "pt": 0, "blk": 0, "e": 0}

                items = []

                def band_block(pi, nh, G, hasprev, qT, B_q, qsel, kfun, B_k, vcol0, rows_cur, rows_prev, orows):
                    nkv = nh // G
                    tiles = [0, 1] if hasprev else [1]
                    st_ = {}
                    ngrp = nh // 4

                    def stageA(hg):
                        if hg == 0:
                            vs = {}
                            for t in tiles:
                                s_ = cnt["v"] % NV
                                cnt["v"] += 1
                                rows = rows_prev if t == 0 else rows_cur
                                sy.dma("sp", vaug[s_][:, :nkv, 0:64],
                                       vz[rows, vcol0:vcol0 + nkv * 64].rearrange("p (h e) -> p h e", e=64), B_vaug[s_], True)
                                vs[t] = s_
                            st_["vs"] = vs
                            st_["ab"] = cnt["blk"] % 2
                            cnt["blk"] += 1
                        ss = cnt["s"] % 2
                        cnt["s"] += 1
                        i = 0
                        for t in (0, 1):
                            tk = t if hasprev else 1
                            for hh in range(4):
                                h = hg * 4 + hh
                                hp = (h % 2) * 64
                                col = (h % 2) * 512 + (t * 2 + hh // 2) * 128
                                f = lambda e, tk=tk, col=col, h=h, hp=hp: e.matmul(
                                    pS[ss][:, col:col + 128], lhsT=kfun(h, tk), rhs=qT[hp:hp + 64, h // 2, qsel], start=True, stop=True)
                                i += 1
                                if i < 8:
                                    sy.pe_nosig(f, [B_q] + B_k, [B_pS[ss]])
                                else:
                                    sy.op("pe", f, [B_q] + B_k, [B_pS[ss]])
                        es = cnt["e"] % 2
                        cnt["e"] += 1
                        pt = cnt["pt"] % 3
                        cnt["pt"] += 1
                        st_[("pt", hg)] = pt
                        sy.op("act", lambda e: e.activation(out=Et[es][:], in_=pS[ss][:], func=AF.Exp, scale=0.125),
                              [B_pS[ss]], [B_Et[es]])
                        sy.op("dve", lambda e: e.tensor_tensor(out=PTt[pt][:], in0=Et[es][:], in1=EB[:, pi * 3 + hg, :], op=ALU.mult),
                              [B_Et[es], B_EB], [B_PTt[pt]])

                    def stageB(hg):
                        vs, ab, pt = st_["vs"], st_["ab"], st_[("pt", hg)]
                        for hh in range(4):
                            h = hg * 4 + hh
                            for ti, t in enumerate(tiles):
                                c0 = (h % 2) * 512 + (t * 2 + hh // 2) * 128
                                f = lambda e, t=t, h=h, c0=c0, ti=ti: e.matmul(
                                    pacc[ab][:, h // 6, (h % 6) * 65:(h % 6) * 65 + 65], lhsT=PTt[pt][:, c0:c0 + 128],
                                    rhs=vaug[vs[t]][:, h // G, :], start=(ti == 0), stop=(ti == len(tiles) - 1))
                                last = (hh == 3 and ti == len(tiles) - 1)
                                rb = [B_PTt[pt]] + [B_vaug[vs[x]] for x in tiles]
                                if not last:
                                    sy.pe_nosig(f, rb, [B_pacc[ab]])
                                else:
                                    sy.op("pe", f, rb, [B_pacc[ab]])
                        if hg != ngrp - 1:
                            return
                        nb = nh * 65
                        n0 = min(nb, 390)
                        sy.op("act", lambda e: e.copy(out=ostg[ab][:, 0:n0], in_=pacc[ab][:, 0, 0:n0]), [B_pacc[ab]], [B_ostg[ab]])
                        if nb > 390:
                            sy.op("dve", lambda e: e.tensor_copy(out=ostg[ab][:, 390:nb], in_=pacc[ab][:, 1, 0:nb - 390]),
                                  [B_pacc[ab]], [B_ostg[ab]])
                        sy.dma("sp", OB[pi, orows, 0:nb], ostg[ab][:, 0:nb], B_ostg[ab], False)

                    for hg in range(ngrp):
                        items.append(("step", (lambda hg=hg: stageA(hg)), (lambda hg=hg: stageB(hg))))

                for sp in range(NSP):
                    t0 = sp * 2048
                    slot = sp % 2

                    def span_loads(sp=sp, t0=t0, slot=slot):
                        sy.dma("sp", aqT[:], qkT[AQ0:AQ0 + 768, t0:t0 + 2048].rearrange("(c p) t -> p c t", p=128), B_aqT, True)
                        sy.dma("sp", bqT[:], qkT[BQ0:BQ0 + 768, t0:t0 + 2048].rearrange("(c p) t -> p c t", p=128), B_bqT, True)
                        sy.dma("sp", bkT[:, :, slot, :], qkT[BK0:BK0 + 768, t0:t0 + 2048].rearrange("(c p) t -> p c t", p=128),
                               B_bkT[slot], True)
                        a0 = 0 if sp > 0 else 128
                        for half in range(2):
                            sy.dma("sp", akT[half * 64:(half + 1) * 64, :, a0:2176],
                                   qkT[AK0:AK0 + 192, t0 - 128 + a0:t0 + 2048].rearrange("(g p) t -> p g t", p=64), B_akT, True)
                    items.append(("load", span_loads))
                    for i in range(16):
                        hasprev = (sp > 0 or i > 0)
                        qsel = slice(128 * i, 128 * i + 128)

                        def kfunA(h, t, i=i):
                            hp = (h % 2) * 64
                            c = 128 * i + 128 * t
                            return akT[hp:hp + 64, h // 4, c:c + 128]
                        rc = slice(t0 + 128 * i, t0 + 128 * i + 128)
                        rp = slice(t0 + 128 * i - 128, t0 + 128 * i)
                        band_block(0, 12, 4, hasprev, aqT, B_aqT, qsel, kfunA, [B_akT], AV0, rc, rp, rc)
                    for i in range(16):
                        hasprev = (sp > 0 or i > 0)
                        qsel = slice(128 * i, 128 * i + 128)

                        def kfunB1(h, t, i=i, slot=slot):
                            hp = (h % 2) * 64
                            if t == 1:
                                return bkT[hp:hp + 64, h // 2, slot, 128 * i:128 * i + 128]
                            if i > 0:
                                return bkT[hp:hp + 64, h // 2, slot, 128 * i - 128:128 * i]
                            return bkT[hp:hp + 64, h // 2, 1 - slot, 1920:2048]
                        rc = slice(t0 + 128 * i, t0 + 128 * i + 128)
                        rp = slice(t0 + 128 * i - 128, t0 + 128 * i)
                        band_block(1, 12, 1, hasprev, bqT, B_bqT, qsel, kfunB1, B_bkT, BV0, rc, rp, rc)
                    for n4 in range(4):
                        for r in range(4):
                            hasprev = (sp > 0 or n4 > 0)
                            qsel = slice(512 * n4 + r, 512 * n4 + 512, 4)

                            def kfunB4(h, t, n4=n4, r=r, slot=slot):
                                hp = (h % 2) * 64
                                if t == 1:
                                    return bkT[hp:hp + 64, h // 2, slot, 512 * n4 + r:512 * n4 + 512:4]
                                if n4 > 0:
                                    return bkT[hp:hp + 64, h // 2, slot, 512 * (n4 - 1) + r:512 * n4:4]
                                return bkT[hp:hp + 64, h // 2, 1 - slot, 1536 + r:2048:4]
                            b0 = t0 + 512 * n4 + r
                            rc = slice(b0, b0 + 509, 4)
                            rp = slice(b0 - 512, b0 - 3, 4)
                            band_block(2, 12, 1, hasprev, bqT, B_bqT, qsel, kfunB4, B_bkT, BV0, rc, rp, rc)
                    for r in range(16):
                        hasprev = (sp > 0)
                        qsel = slice(r, 2048, 16)

                        def kfunB16(h, t, r=r, slot=slot):
                            hp = (h % 2) * 64
                            sl = slot if t == 1 else 1 - slot
                            return bkT[hp:hp + 64, h // 2, sl, r:2048:16]
                        b0 = t0 + r
                        rc = slice(b0, b0 + 2033, 16)
                        rp = slice(b0 - 2048, b0 - 15, 16)
                        band_block(3, 12, 1, hasprev, bqT, B_bqT, qsel, kfunB16, B_bkT, BV0, rc, rp, rc)
                pending = None
                for it in items:
                    if it[0] == "load":
                        it[1]()
                    else:
                        it[1]()
                        if pending is not None:
                            pending()
                        pending = it[2]
                if pending is not None:
                    pending()
                sy.barrier()
            if stop < 3:
                continue

            with contextlib.ExitStack() as ph:
                strip = sb(ph, "strip", [128, 4, CSTRIP], BF16)
                B_strip = Buf("strip")
                QT = [sb(ph, "cQT%d" % i, [128, S], BF16) for i in range(2)]
                K1 = [sb(ph, "cK1%d" % i, [128, S], BF16) for i in range(2)]
                K2 = [sb(ph, "cK2%d" % i, [128, S], BF16) for i in range(2)]
                VA = [sb(ph, "cVA%d" % i, [128, NT, 129], BF16) for i in range(2)]
                B_QT = [Buf("cQT%d" % i) for i in range(2)]
                B_K1 = [Buf("cK1%d" % i) for i in range(2)]
                B_K2 = [Buf("cK2%d" % i) for i in range(2)]
                B_VA = [Buf("cVA%d" % i) for i in range(2)]
                NE, NP = 2, 3
                Ec = [sb(ph, "Ec%d" % i, [128, 1024], BF16) for i in range(NE)]
                B_Ec = [Buf("Ec%d" % i) for i in range(NE)]
                Pc = [sb(ph, "Pc%d" % i, [128, 1024], BF16) for i in range(NP)]
                B_Pc = [Buf("Pc%d" % i) for i in range(NP)]
                Y1 = sb(ph, "Y1", [128, 4, 128], F32)
                B_Y1 = [Buf("Y1_%d" % i) for i in range(4)]
                yd = [sb(ph, "yd%d" % i, [128, 128], F32) for i in range(2)]
                B_yd = [Buf("yd%d" % i) for i in range(2)]
                sm = [sb(ph, "sm%d" % i, [128, 8], F32) for i in range(2)]
                B_sm = [Buf("sm%d" % i) for i in range(2)]
                cjunk = sb(ph, "cjunk", [128, 128], F32)
                B_cjunk = Buf("cjunk")
                ystg = [sb(ph, "ystg%d" % i, [128, 4, 128], F32) for i in range(2)]
                B_ystg = [Buf("ystg%d" % i) for i in range(2)]
                NSB = 2
                pSc = [ps(ph, "pSc%d" % i, [128, 1024], F32) for i in range(NSB)]
                B_pSc = [Buf("pSc%d" % i) for i in range(NSB)]
                pA = [ps(ph, "pA%d" % i, [128, 512], F32) for i in range(4)]
                B_pA = [Buf("pA%d" % i) for i in range(4)]

                sy.dma("sp", strip[:], ebC, B_strip, True)
                for i in range(2):
                    sy.op("pool", lambda e, i=i: e.memset(K1[i][64:128, :], 0.0), [], [B_K1[i]])
                    sy.op("pool", lambda e, i=i: e.memset(K2[i][0:64, :], 0.0), [], [B_K2[i]])
                    sy.op("pool", lambda e, i=i: e.memset(VA[i][:, :, 128:129], 1.0), [], [B_VA[i]])

                def load_head(hc):
                    b = hc % 2
                    sy.dma("sp", QT[b][:], qkT[CQ0 + hc * 128:CQ0 + (hc + 1) * 128, :], B_QT[b], True)
                    sy.dma("sp", K1[b][0:64, :], qkT[CK0 + hc * 128:CK0 + hc * 128 + 64, :], B_K1[b], True)
                    sy.dma("sp", K2[b][64:128, :], qkT[CK0 + hc * 128 + 64:CK0 + (hc + 1) * 128, :], B_K2[b], True)
                    nchunk = max(1, NT // 16)
                    tpc = NT // nchunk
                    for ci in range(nchunk):
                        sy.dma("sp", VA[b][:, ci * tpc:(ci + 1) * tpc, 0:128],
                               vz[ci * tpc * 128:(ci + 1) * tpc * 128, CV0 + hc * 128:CV0 + (hc + 1) * 128].rearrange("(t p) e -> p t e", p=128),
                               B_VA[b], True)

                accs = [sb(ph, "accs%d" % i, [128, 4, 129], F32) for i in range(2)]
                B_accs = [Buf("accs%d" % i) for i in range(2)]
                mhalf = sb(ph, "mhalf", [128, 1], F32)
                sy.op("pool", lambda e: e.memset(mhalf[:], -0.5), [], [B_const])

                load_head(0)
                steps = []
                for hc in range(4):
                    for qc in range(NST):
                        for m in range(2):
                            kt = 0
                            while kt < 4 * qc + 4:
                                if kt + 1 < 4 * qc:
                                    steps.append((hc, qc, m, [kt, kt + 1]))
                                    kt += 2
                                else:
                                    steps.append((hc, qc, m, [kt]))
                                    kt += 1
                LOOK = 1
                cix = {"e": 0, "g": 0}
                loaded = {0}

                def geom(qc, kt):
                    q0 = max(qc * 512, kt * 128)
                    return q0, (qc + 1) * 512 - q0, q0 // 128 - kt

                def stageA(i):
                    hc, qc, m, kts = steps[i]
                    b = hc % 2
                    KP, B_KP = (K1[b], B_K1[b]) if m == 0 else (K2[b], B_K2[b])
                    s_ = i % NSB
                    p = i % NP
                    for j, kt in enumerate(kts):
                        q0, nq, dmin = geom(qc, kt)
                        f = lambda e, j=j, kt=kt, q0=q0, nq=nq: e.matmul(pSc[s_][:, j * 512:j * 512 + nq], lhsT=KP[:, kt * 128:(kt + 1) * 128],
                                                                        rhs=QT[b][:, q0:q0 + nq], start=True, stop=True)
                        if j < len(kts) - 1:
                            sy.pe_nosig(f, [B_KP, B_QT[b]], [B_pSc[s_]])
                        else:
                            sy.op("pe", f, [B_KP, B_QT[b]], [B_pSc[s_]])
                    q0l, nql, dminl = geom(qc, kts[-1])
                    width = nql if len(kts) == 1 else 1024
                    if dminl >= CNEAR:
                        sy.op("act", lambda e: e.activation(out=Pc[p][:, :width], in_=pSc[s_][:, :width], func=AF.Exp, scale=0.125),
                              [B_pSc[s_]], [B_Pc[p]])
                    else:
                        ei = cix["e"] % NE
                        cix["e"] += 1
                        sy.op("act", lambda e: e.activation(out=Ec[ei][:, :width], in_=pSc[s_][:, :width], func=AF.Exp, scale=0.125),
                              [B_pSc[s_]], [B_Ec[ei]])
                        for j, kt in enumerate(kts):
                            q0, nq, dmin = geom(qc, kt)
                            x0 = q0 - kt * 128 + 384
                            sy.op("dve", lambda e, j=j, nq=nq, x0=x0: e.tensor_tensor(
                                out=Pc[p][:, j * 512:j * 512 + nq], in0=Ec[ei][:, j * 512:j * 512 + nq], in1=strip[:, hc, x0:x0 + nq],
                                op=ALU.mult), [B_Ec[ei], B_strip], [B_Pc[p]])

                def stageB(i):
                    hc, qc, m, kts = steps[i]
                    b = hc % 2
                    if (hc + 1) not in loaded and hc + 1 < 4:
                        loaded.add(hc + 1)
                        load_head(hc + 1)
                    p = i % NP
                    mms = []
                    for j, kt in enumerate(kts):
                        q0, nq, dmin = geom(qc, kt)
                        for qt in range(4 * qc, 4 * qc + 4):
                            if qt >= kt:
                                mms.append((j, kt, qt, qt - q0 // 128))
                    wl = sorted(set(qt % 4 for (_, _, qt, _) in mms))
                    for mi, (j, kt, qt, jj) in enumerate(mms):
                        a = qt % 4
                        f = lambda e, a=a, j=j, jj=jj, qt=qt, kt=kt: e.matmul(
                            pA[a][:, 0:129], lhsT=Pc[p][:, j * 512 + jj * 128:j * 512 + (jj + 1) * 128],
                            rhs=VA[b][:, kt, :], start=(kt == 0), stop=(kt == qt))
                        if mi < len(mms) - 1:
                            sy.pe_nosig(f, [B_Pc[p], B_VA[b]], [B_pA[a]])
                        else:
                            sy.op("pe", f, [B_Pc[p], B_VA[b]], [B_pA[x] for x in wl])
                    kt = kts[-1]
                    if kt != 4 * qc + 3:
                        return
                    g = cix["g"] % 2
                    cix["g"] += 1
                    AC, B_AC = accs[g], B_accs[g]
                    yb = (hc * NST + qc) % 2
                    for a in range(4):
                        sy.op("dve", lambda e, a=a: e.tensor_copy(out=AC[:, a, :], in_=pA[a][:, 0:129]), [B_pA[a]], [B_AC])
                    for a in range(4):
                        smb, B_smb = sm[a % 2], B_sm[a % 2]
                        sy.op("dve", lambda e, a=a, smb=smb: e.reciprocal(out=smb[:, 0:1], in_=AC[:, a, 128:129]), [B_AC], [B_smb])
                        if m == 0:
                            sy.op("dve", lambda e, a=a, smb=smb: e.tensor_scalar(out=Y1[:, a, :], in0=AC[:, a, 0:128], scalar1=smb[:, 0:1],
                                                                                 scalar2=None, op0=ALU.mult), [B_AC, B_smb], [B_Y1[a]])
                        else:
                            ydb, B_ydb = yd[a % 2], B_yd[a % 2]
                            sy.op("dve", lambda e, smb=smb: e.tensor_tensor(out=smb[:, 1:2], in0=smb[:, 0:1], in1=neglam[:, l:l + 1],
                                                                            op=ALU.mult), [B_smb, B_const], [B_smb])
                            sy.op("dve", lambda e, a=a, smb=smb, ydb=ydb: e.scalar_tensor_tensor(
                                out=ydb[:], in0=AC[:, a, 0:128], scalar=smb[:, 1:2], in1=Y1[:, a, :], op0=ALU.mult, op1=ALU.add),
                                [B_AC, B_smb, B_Y1[a]], [B_ydb])
                            sy.op("dve", lambda e, smb=smb, ydb=ydb: e.scalar_tensor_tensor(
                                out=cjunk[:], in0=ydb[:], scalar=1.0, in1=ydb[:], op0=ALU.mult, op1=ALU.mult, accum_out=smb[:, 2:3]),
                                [B_ydb], [B_cjunk, B_smb])
                            sy.op("dve", lambda e, smb=smb: e.tensor_scalar(out=smb[:, 3:4], in0=smb[:, 2:3], scalar1=1.0 / 128, scalar2=EPS,
                                                                            op0=ALU.mult, op1=ALU.add), [B_smb], [B_smb])
                            sy.op("pool", lambda e, smb=smb: e.tensor_tensor(out=smb[:, 5:6], in0=smb[:, 3:4], in1=mhalf[:], op=ALU.pow),
                                  [B_smb, B_const], [B_smb])
                            sy.op("dve", lambda e, a=a, smb=smb, ydb=ydb: e.scalar_tensor_tensor(
                                out=ystg[yb][:, a, :], in0=ydb[:], scalar=smb[:, 5:6], in1=gsubb[:, l, :], op0=ALU.mult, op1=ALU.mult),
                                [B_ydb, B_smb, B_const], [B_ystg[yb]])
                    if m == 1:
                        sy.dma("sp", YC[qc * 512:(qc + 1) * 512, hc * 128:(hc + 1) * 128].rearrange("(j p) e -> p j e", p=128),
                               ystg[yb][:], B_ystg[yb], False)

                nsteps = len(steps)
                for i in range(nsteps + LOOK):
                    if i < nsteps:
                        stageA(i)
                    if i >= LOOK:
                        stageB(i - LOOK)
                sy.barrier()
            if stop < 4:
                continue

            with contextlib.ExitStack() as ph:
                WO = sb(ph, "WO", [128, 16, D], BF16)
                B_WO = Buf("WO")
                gt = sb(ph, "p4gt", [128, D], F32)
                B_gt = Buf("p4gt")
                mh4 = sb(ph, "mh4", [128, 1], F32)
                B_mh4 = Buf("mh4")

                def dbl(name, shape, dt, n=2):
                    return [sb(ph, "%s%d" % (name, i), shape, dt) for i in range(n)], [Buf("%s%d" % (name, i)) for i in range(n)]
                oa, B_oa = dbl("oa", [128, 12, 65], F32)
                ob, B_ob = dbl("ob", [128, 3, 780], F32)
                yct, B_yct = dbl("yct", [128, 512], F32)
                zt, B_zt = dbl("zt", [128, 2048], BF16)
                xr, B_xr = dbl("xr", [128, D], F32)
                obs, B_obs = dbl("obs", [128, 12, 65], F32)
                rr, B_rr = dbl("rr", [128, 2, 12], F32)
                yy, B_yy = dbl("yy", [128, 2048], F32)
                szt, B_szt = dbl("szt", [128, 2048], F32)
                yg, B_yg = dbl("yg", [128, 2048], BF16)
                ygT, B_ygT = dbl("ygT", [128, 16, 128], BF16)
                fs, B_fs = dbl("fs", [128, 4], F32)
                tn, B_tn = dbl("tn", [128, D], F32)
                og, B_og = dbl("og", [128, D], F32)
                fj = sb(ph, "fj", [128, D], BF16)
                B_fj = Buf("fj")
                pTy = [ps(ph, "pTy%d" % i, [128, 16, 128], BF16) for i in range(2)]
                B_pTy = [Buf("pTy%d" % i) for i in range(2)]
                po = [ps(ph, "po%d" % i, [128, 1024], F32) for i in range(2)]
                B_po = [Buf("po%d" % i) for i in range(2)]

                for k in range(16):
                    sy.dma("sp", WO[:, k, :], wbf_out[l, k * 128:(k + 1) * 128, :], B_WO, True)
                sy.dma("sp", gt[:], modb[l, 2], B_gt, True)
                sy.op("pool", lambda e: e.memset(mh4[:], -0.5), [], [B_mh4])

                def L1(tt):
                    s = tt % 2
                    rows = slice(tt * 128, (tt + 1) * 128)
                    sy.dma("sp", oa[s][:].rearrange("p h e -> p (h e)"), OB[0, rows, :], B_oa[s], True)
                    sy.dma("sp", ob[s][:], OB[1:4, rows, :].rearrange("c p f -> p c f"), B_ob[s], True)
                    sy.dma("sp", yct[s][:], YC[rows, :], B_yct[s], True)
                    sy.dma("sp", zt[s][:], vz[rows, Z0:Z0 + 2048], B_zt[s], True)

                def L3(tt):
                    s = tt % 2
                    sy.dma("sp", xr[s][:], x_src[tt * 128:(tt + 1) * 128, :], B_xr[s], True)

                def S1(tt):
                    s = tt % 2
                    obv = lambda c: ob[s][:, c, :].rearrange("p (h e) -> p h e", e=65)
                    sy.op("act", lambda e: e.activation(out=szt[s][:], in_=zt[s][:], func=AF.Silu), [B_zt[s]], [B_szt[s]])
                    sy.op("pool", lambda e: e.tensor_tensor(out=obs[s][:], in0=obv(0), in1=obv(1), op=ALU.add), [B_ob[s]], [B_obs[s]])
                    sy.op("pool", lambda e: e.tensor_tensor(out=obs[s][:], in0=obs[s][:], in1=obv(2), op=ALU.add), [B_ob[s], B_obs[s]], [B_obs[s]])
                    sy.op("dve", lambda e: e.tensor_tensor(out=rr[s][:, 0, :], in0=oa[s][:, :, 64], in1=esink[:, l, :], op=ALU.add),
                          [B_oa[s], B_const], [B_rr[s]])
                    sy.op("dve", lambda e: e.reciprocal(out=rr[s][:, 0, :], in_=rr[s][:, 0, :]), [B_rr[s]], [B_rr[s]])
                    sy.op("dve", lambda e: e.tensor_tensor(out=yy[s][:, 0:768].rearrange("p (h e) -> p h e", e=64), in0=oa[s][:, :, 0:64],
                                                           in1=rr[s][:, 0, :].unsqueeze(2).to_broadcast([128, 12, 64]), op=ALU.mult),
                          [B_oa[s], B_rr[s]], [B_yy[s]])
                    sy.op("dve", lambda e: e.tensor_tensor(out=yg[s][:, 0:768], in0=yy[s][:, 0:768], in1=szt[s][:, 0:768], op=ALU.mult),
                          [B_yy[s], B_szt[s]], [B_yg[s]])
                    sy.op("dve", lambda e: e.reciprocal(out=rr[s][:, 1, :], in_=obs[s][:, :, 64]), [B_obs[s]], [B_rr[s]])
                    sy.op("pool", lambda e: e.tensor_tensor(out=yy[s][:, 768:1536].rearrange("p (h e) -> p h e", e=64), in0=obs[s][:, :, 0:64],
                                                            in1=rr[s][:, 1, :].unsqueeze(2).to_broadcast([128, 12, 64]), op=ALU.mult),
                          [B_obs[s], B_rr[s]], [B_yy[s]])
                    sy.op("dve", lambda e: e.tensor_tensor(out=yg[s][:, 768:1536], in0=yy[s][:, 768:1536], in1=szt[s][:, 768:1536], op=ALU.mult),
                          [B_yy[s], B_szt[s]], [B_yg[s]])
                    sy.op("pool", lambda e: e.tensor_tensor(out=yg[s][:, 1536:2048], in0=yct[s][:], in1=szt[s][:, 1536:2048], op=ALU.mult),
                          [B_yct[s], B_szt[s]], [B_yg[s]])

                def T2(tt):
                    s = tt % 2
                    for k in range(16):
                        f = lambda e, k=k: e.transpose(out=pTy[s][:, k, :], in_=yg[s][:, k * 128:(k + 1) * 128], identity=ident[:])
                        if k % 8 < 7:
                            sy.pe_nosig(f, [B_yg[s], B_const], [B_pTy[s]])
                        else:
                            sy.op("pe", f, [B_yg[s], B_const], [B_pTy[s]])
                    sy.op("act", lambda e: e.copy(out=ygT[s][:, 0:8, :], in_=pTy[s][:, 0:8, :]), [B_pTy[s]], [B_ygT[s]])
                    sy.op("act", lambda e: e.copy(out=ygT[s][:, 8:16, :], in_=pTy[s][:, 8:16, :]), [B_pTy[s]], [B_ygT[s]])

                def M2(tt):
                    s = tt % 2
                    for n in range(2):
                        for k in range(16):
                            f = lambda e, k=k, n=n: e.matmul(po[s][:, n * 512:(n + 1) * 512], lhsT=ygT[s][:, k, :],
                                                             rhs=WO[:, k, n * 512:(n + 1) * 512], start=(k == 0), stop=(k == 15))
                            if k < 15 or n == 0:
                                sy.pe_nosig(f, [B_ygT[s], B_WO], [B_po[s]])
                            else:
                                sy.op("pe", f, [B_ygT[s], B_WO], [B_po[s]])

                def S3(tt):
                    s = tt % 2
                    f4, B4 = fs[s], B_fs[s]
                    sy.op("act", lambda e: e.activation(out=fj[:], in_=po[s][:], func=AF.Square, accum_out=f4[:, 0:1]), [B_po[s]], [B_fj, B4])
                    sy.op("dve", lambda e: e.tensor_scalar(out=f4[:, 1:2], in0=f4[:, 0:1], scalar1=1.0 / D, scalar2=EPS, op0=ALU.mult, op1=ALU.add),
                          [B4], [B4])
                    sy.op("pool", lambda e: e.tensor_tensor(out=f4[:, 3:4], in0=f4[:, 1:2], in1=mh4[:], op=ALU.pow), [B4, B_mh4], [B4])
                    sy.op("dve", lambda e: e.scalar_tensor_tensor(out=tn[s][:], in0=po[s][:], scalar=f4[:, 3:4], in1=gt[:], op0=ALU.mult, op1=ALU.mult),
                          [B_po[s], B4, B_gt], [B_tn[s]])
                    sy.op("pool", lambda e: e.tensor_tensor(out=og[s][:], in0=tn[s][:], in1=xr[s][:], op=ALU.add), [B_tn[s], B_xr[s]], [B_og[s]])
                    sy.dma("sp", x_dst[tt * 128:(tt + 1) * 128, :], og[s][:], B_og[s], False)

                ok = lambda t: 0 <= t < NT
                for i in range(-3, NT + 1):
                    if ok(i + 3):
                        L1(i + 3)
                    if ok(i):
                        L3(i)
                    if ok(i + 2):
                        S1(i + 2)
                    if ok(i + 1):
                        T2(i + 1)
                    if ok(i):
                        M2(i)
                    if ok(i - 1):
                        S3(i - 1)
                sy.barrier()
        print("build done: sems=%d counts=%s" % (sy.nsem, sy.cnt))
    return nc


def _rel_bucket(dist):
    n = np.maximum(dist, 0).astype(np.int64)
    nf = np.maximum(n, 1).astype(np.float32)
    large = 16 + (np.log(nf / np.float32(16.0)) / np.float32(math.log(2048 / 16)) * np.float32(16.0)).astype(np.int32)
    large = np.minimum(large, 31)
    return np.where(n < 16, n, large).astype(np.int64)


def _host_tables(rel_table):
    rel_table = np.asarray(rel_table, dtype=np.float32)
    k = np.arange(128)[:, None]
    q = np.arange(128)[None, :]
    biasAB = np.zeros((128, 12, 2, 2, 2, 128), np.float32)
    maskAB = np.zeros((128, 12, 2, 2, 2, 128), np.float32)
    pats = [(1, 127, 0), (1, 128, 12), (4, 128, 12), (16, 128, 12)]
    for pi, (d, maxd, ho) in enumerate(pats):
        for t in range(2):
            dist = q - k + (128 if t == 0 else 0)
            idx = _rel_bucket(dist * d)
            msk = ((dist >= 0) & (dist <= maxd)).astype(np.float32)
            for hg in range(3):
                for hh in range(4):
                    biasAB[:, pi * 3 + hg, hh % 2, t, hh // 2, :] = rel_table[idx, ho + hg * 4 + hh]
                    maskAB[:, pi * 3 + hg, hh % 2, t, hh // 2, :] = msk
    biasAB = biasAB.reshape(128, 12, 1024)
    maskAB = maskAB.reshape(128, 12, 1024)
    xx = np.arange(CSTRIP)[None, :]
    dist = xx - 384 - k
    idx = _rel_bucket(dist)
    biasC = np.stack([rel_table[idx, 24 + h] for h in range(4)], axis=1).astype(np.float32)
    maskC = (dist >= 0).astype(np.float32)
    b31 = rel_table[31, 24:28].reshape(1, 4).astype(np.float32)
    return biasAB, maskAB, np.ascontiguousarray(biasC), maskC, b31


def _perm_cols():
    sizes = [768, 192, 192, 768, 768, 768, 512, 512, 512, 2048]
    offs = np.cumsum([0] + sizes)
    seg = lambda i: np.arange(offs[i], offs[i + 1])
    order = [0, 1, 3, 4, 6, 7, 2, 5, 8, 9]
    return np.concatenate([seg(i) for i in order])


_NC_CACHE = {}
ACTIVE = {0: 0, 1: 1, 4: 2, 5: 3}


def make_in_maps(inputs, S=8192):
    x = np.asarray(inputs["x"], np.float32)
    c = np.asarray(inputs["c"], np.float32)
    perm = _perm_cols()
    w_in_p = np.ascontiguousarray(np.asarray(inputs["w_in"], np.float32)[:, :, perm])
    biasAB, maskAB, biasC, maskC, b31 = _host_tables(inputs["rel_table"])
    lam4 = np.stack([np.asarray(inputs[k], np.float32) for k in ("lam_q1", "lam_k1", "lam_q2", "lam_k2")], axis=1).reshape(NL, 256)
    shared = {
        "w_in": w_in_p,
        "w_out": np.ascontiguousarray(np.asarray(inputs["w_out"], np.float32)),
        "w_ada": np.ascontiguousarray(np.asarray(inputs["w_ada"], np.float32)),
        "b_ada": np.ascontiguousarray(np.asarray(inputs["b_ada"], np.float32)),
        "g_pre": np.ascontiguousarray(np.asarray(inputs["g_pre"], np.float32)),
        "g_post": np.ascontiguousarray(np.asarray(inputs["g_post"], np.float32)),
        "a_sinks": np.ascontiguousarray(np.asarray(inputs["a_sinks"], np.float32)),
        "lam4": np.ascontiguousarray(lam4),
        "g_sub": np.ascontiguousarray(np.asarray(inputs["g_sub"], np.float32)),
        "biasAB": biasAB, "maskAB": maskAB, "biasC": biasC, "maskC": maskC, "b31": b31,
        "ident": np.eye(128, dtype=np.float32),
    }
    in_maps = []
    zero_shared = None
    for core in range(8):
        b = ACTIVE.get(core)
        if b is None:
            if zero_shared is None:
                zero_shared = {k: np.zeros_like(v) for k, v in shared.items()}
                zero_shared["xb"] = np.zeros((S, D), np.float32)
                zero_shared["c2"] = np.zeros((128, 8), np.float32)
            in_maps.append(zero_shared)
            continue
        m = dict(shared)
        m["xb"] = np.ascontiguousarray(x[b, :S])
        m["c2"] = np.ascontiguousarray(c[b].reshape(8, 128).T)
        in_maps.append(m)
    return in_maps


def kernel(x, c, rel_table, w_in, w_out, w_ada, b_ada, g_pre, g_post, a_sinks,
           lam_q1, lam_k1, lam_q2, lam_k2, g_sub):
    inputs = dict(x=x, c=c, rel_table=rel_table, w_in=w_in, w_out=w_out, w_ada=w_ada, b_ada=b_ada, g_pre=g_pre,
                  g_post=g_post, a_sinks=a_sinks, lam_q1=lam_q1, lam_k1=lam_k1, lam_q2=lam_q2, lam_k2=lam_k2, g_sub=g_sub)
    if "nc" not in _NC_CACHE:
        _NC_CACHE["nc"] = build()
    nc = _NC_CACHE["nc"]
    in_maps = make_in_maps(inputs)
    res = run_bass_kernel_spmd(nc, in_maps, core_ids=list(range(8)))
    core_of = {b: core for core, b in ACTIVE.items()}
    outs = [np.asarray(res.results[core_of[b]]["out"], np.float32) for b in range(4)]
    return np.stack(outs, axis=0)
```

```python
import math
import contextlib
import numpy as np
import concourse.bass as bass
import concourse.mybir as mybir
from concourse.bass_utils import run_bass_kernel_spmd

F32 = mybir.dt.float32
BF16 = mybir.dt.bfloat16
AF = mybir.ActivationFunctionType
ALU = mybir.AluOpType
AX = mybir.AxisListType

D = 1024
NL = 2
EPS = 1e-6
NQK = 3520
NVZ = 3520
AQ0, AK0, BQ0, BK0, CQ0, CK0 = 0, 768, 960, 1728, 2496, 3008
AV0, BV0, CV0, Z0 = 0, 192, 960, 1472
CSTRIP = 2560
CNEAR = 13
LAM_INIT = [0.8 - 0.6 * math.exp(-0.3 * l) for l in range(NL)]


class Buf:
    __slots__ = ("name", "w", "r", "dsem", "dcnt")

    def __init__(self, name):
        self.name = name
        self.w = []
        self.r = []
        self.dsem = None
        self.dcnt = 0


class Sync:
    def __init__(self, nc, stack):
        self.nc = nc
        self.stack = stack
        self.eng = {"pe": nc.tensor, "act": nc.scalar, "dve": nc.vector, "pool": nc.gpsimd, "sp": nc.sync}
        self.sem = {}
        for k in ("pe", "act", "dve", "pool"):
            self.sem[k] = stack.enter_context(nc.semaphore("e_" + k))
        self.cnt = {k: 0 for k in self.sem}
        self.seen = {k: {} for k in self.eng}
        self.dbufs = []
        self.dcount = {}
        self.nsem = 4

    def _wait(self, e, ev):
        key, val = ev
        if key == "pe" and e == "pe":
            return
        if self.seen[e].get(key, 0) >= val:
            return
        self.eng[e].wait_ge(self.sem[key], val)
        self.seen[e][key] = val

    def _deps(self, e, reads, writes):
        for b in reads:
            for ev in b.w:
                self._wait(e, ev)
        for b in writes:
            for ev in b.w:
                self._wait(e, ev)
            for ev in b.r:
                self._wait(e, ev)

    def op(self, e, fn, reads=(), writes=()):
        self._deps(e, reads, writes)
        ins = fn(self.eng[e])
        self.cnt[e] += 1
        ins.then_inc(self.sem[e], 1)
        ev = (e, self.cnt[e])
        for b in reads:
            b.r = [x for x in b.r if x[0] != e] + [ev]
        for b in writes:
            b.w = [x for x in b.w if x[0] != e] + [ev]
            b.r = []
        return ins

    def pe_nosig(self, fn, reads=(), writes=()):
        self._deps("pe", reads, writes)
        return fn(self.eng["pe"])

    def _dsem(self, b):
        if b.dsem is None:
            key = "d_" + b.name
            if key not in self.sem:
                self.sem[key] = self.stack.enter_context(self.nc.semaphore(key))
                self.dcount[key] = 0
                self.nsem += 1
            b.dsem = key
            b.dcnt = self.dcount[key]
            self.dbufs.append(b)
        return b.dsem

    def dma(self, q, out_ap, in_ap, buf, load, extra_reads=(), **kw):
        key = self._dsem(buf)
        if load:
            self._deps(q, extra_reads, [buf])
        else:
            self._deps(q, [buf] + list(extra_reads), [])
        ins = self.eng[q].dma_start(out=out_ap, in_=in_ap, **kw)
        buf.dcnt += 16
        self.dcount[key] = buf.dcnt
        ins.then_inc(self.sem[key], 16)
        ev = (key, buf.dcnt)
        if load:
            buf.w = [x for x in buf.w if x[0] != key] + [ev]
            buf.r = []
        else:
            buf.r = [x for x in buf.r if x[0] != key] + [ev]
        return ins

    def barrier(self):
        evs = [(k, self.cnt[k]) for k in ("pe", "act", "dve", "pool") if self.cnt[k] > 0]
        evs += [(k, v) for k, v in self.dcount.items() if v > 0]
        for e in self.eng:
            for ev in evs:
                if ev[0] == e:
                    continue
                self._wait(e, ev)
        for b in self.dbufs:
            b.w = []
            b.r = []
        self.dbufs = []


def build(S=8192, nl=NL, dbg=False, stop=99):
    NT = S // 128
    NST = S // 512
    NSP = S // 2048
    nc = bass.Bass("TRN2", target_bir_lowering=False)

    def din(name, shape):
        return nc.dram_tensor(name, shape, F32, kind="ExternalInput").ap()

    x_in = din("xb", [S, D])
    c_in = din("c2", [128, 8])
    w_in = din("w_in", [NL, D, 7040])
    w_out = din("w_out", [NL, 2048, D])
    w_ada = din("w_ada", [NL, D, 3072])
    b_ada = din("b_ada", [NL, 3072])
    g_pre = din("g_pre", [NL, D])
    g_post = din("g_post", [NL, D])
    a_sinks = din("a_sinks", [NL, 12])
    lam4 = din("lam4", [NL, 256])
    g_sub = din("g_sub", [NL, 128])
    biasAB = din("biasAB", [128, 12, 1024])
    maskAB = din("maskAB", [128, 12, 1024])
    biasC = din("biasC", [128, 4, CSTRIP])
    maskC = din("maskC", [128, CSTRIP])
    b31_in = din("b31", [1, 4])
    ident_in = din("ident", [128, 128])
    out = nc.dram_tensor("out", [S, D], F32, kind="ExternalOutput").ap()

    skind = "ExternalOutput" if dbg else "Internal"

    def scratch(name, shape, dt):
        return nc.dram_tensor(name, shape, dt, kind=skind).ap()

    wbf_in = scratch("wbf_in", [NL, D, 7040], BF16)
    wbf_out = scratch("wbf_out", [NL, 2048, D], BF16)
    modb = scratch("modb", [NL, 3, 128, D], F32)
    ebAB = scratch("ebAB", [128, 12, 1024], BF16)
    ebC = scratch("ebC", [128, 4, CSTRIP], BF16)
    qkT = scratch("qkT", [NQK, S], BF16)
    vz = scratch("vz", [S, NVZ], BF16)
    OB = scratch("OB", [4, S, 780], F32)
    YC = scratch("YC", [S, 512], F32)
    x1 = scratch("x1", [S, D], F32)

    with contextlib.ExitStack() as top:
        sy = Sync(nc, top)

        uid = [0]

        def sb(stack, name, shape, dt):
            uid[0] += 1
            return stack.enter_context(nc.sbuf_tensor("s%d_%s" % (uid[0], name), shape, dt))

        def ps(stack, name, shape, dt):
            uid[0] += 1
            return stack.enter_context(nc.psum_tensor("p%d_%s" % (uid[0], name), shape, dt))

        ident = sb(top, "ident", [128, 128], BF16)
        neglam = sb(top, "neglam", [128, NL], F32)
        esink = sb(top, "esink", [128, NL, 12], F32)
        gsubb = sb(top, "gsubb", [128, NL, 128], F32)
        b31 = sb(top, "b31", [128, 4], F32)
        B_const = Buf("const")

        with contextlib.ExitStack() as ph:
            identf = sb(ph, "identf", [128, 128], F32)
            ones = sb(ph, "ones", [128, 128], F32)
            cin = sb(ph, "cin", [128, 8], F32)
            cact = sb(ph, "cact", [128, 8], F32)
            crep = sb(ph, "crep", [128, 8, 128], F32)
            lamt = sb(ph, "lamt", [128, NL, 256], F32)
            lprod = sb(ph, "lprod", [128, 2, 64], F32)
            lsum = sb(ph, "lsum", [128, 2], F32)
            lexp = sb(ph, "lexp", [128, 2], F32)
            ldiff = sb(ph, "ldiff", [128, 1], F32)
            sinkt = sb(ph, "sinkt", [128, NL, 12], F32)
            gsubt = sb(ph, "gsubt", [128, NL, 128], F32)
            B_identf, B_ones, B_cin, B_cact, B_crep = Buf("identf"), Buf("ones"), Buf("cin"), Buf("cact"), Buf("crep")
            B_lamt, B_lprod, B_lsum, B_lexp, B_ldiff = Buf("lamt"), Buf("lprod"), Buf("lsum"), Buf("lexp"), Buf("ldiff")
            B_sinkt, B_gsubt, B_b31 = Buf("sinkt"), Buf("gsubt"), Buf("b31")

            sy.dma("sp", identf[:], ident_in, B_identf, True)
            sy.op("dve", lambda e: e.tensor_copy(out=ident[:], in_=identf[:]), [B_identf], [B_const])
            sy.op("pool", lambda e: e.memset(ones[:], 1.0), [], [B_ones])
            sy.dma("sp", cin[:], c_in, B_cin, True)
            sy.op("act", lambda e: e.activation(out=cact[:], in_=cin[:], func=AF.Silu), [B_cin], [B_cact])
            for k in range(8):
                sy.op("dve", lambda e, k=k: e.tensor_scalar(out=crep[:, k, :], in0=ones[:], scalar1=cact[:, k:k + 1],
                                                            scalar2=None, op0=ALU.mult), [B_ones, B_cact], [B_crep])
            sy.dma("sp", b31[:], b31_in[0].partition_broadcast(128), B_b31, True)
            sy.dma("sp", lamt[:].rearrange("p l f -> p (l f)"),
                   lam4.rearrange("l f -> (l f)").partition_broadcast(128), B_lamt, True)
            sy.dma("sp", sinkt[:].rearrange("p l f -> p (l f)"),
                   a_sinks.rearrange("l f -> (l f)").partition_broadcast(128), B_sinkt, True)
            sy.dma("sp", gsubt[:].rearrange("p l f -> p (l f)"),
                   g_sub.rearrange("l f -> (l f)").partition_broadcast(128), B_gsubt, True)
            sy.op("act", lambda e: e.activation(out=esink[:].rearrange("p l f -> p (l f)"),
                                                in_=sinkt[:].rearrange("p l f -> p (l f)"), func=AF.Exp),
                  [B_sinkt], [B_const])
            for l in range(NL):
                lt = lamt[:, l, :].rearrange("p (a f) -> p a f", a=4)
                sy.op("dve", lambda e, lt=lt: e.tensor_tensor(out=lprod[:, 0, :], in0=lt[:, 0, :], in1=lt[:, 1, :], op=ALU.mult),
                      [B_lamt], [B_lprod])
                sy.op("dve", lambda e, lt=lt: e.tensor_tensor(out=lprod[:, 1, :], in0=lt[:, 2, :], in1=lt[:, 3, :], op=ALU.mult),
                      [B_lamt], [B_lprod])
                sy.op("dve", lambda e: e.reduce_sum(out=lsum[:], in_=lprod[:], axis=AX.X), [B_lprod], [B_lsum])
                sy.op("act", lambda e: e.activation(out=lexp[:], in_=lsum[:], func=AF.Exp), [B_lsum], [B_lexp])
                sy.op("dve", lambda e: e.tensor_tensor(out=ldiff[:], in0=lexp[:, 1:2], in1=lexp[:, 0:1], op=ALU.subtract),
                      [B_lexp], [B_ldiff])
                sy.op("dve", lambda e, l=l: e.tensor_scalar(out=neglam[:, l:l + 1], in0=ldiff[:], scalar1=-LAM_INIT[l],
                                                            scalar2=None, op0=ALU.add), [B_ldiff], [B_const])
                sy.op("dve", lambda e, l=l: e.tensor_scalar(out=gsubb[:, l, :], in0=gsubt[:, l, :], scalar1=1.0 - LAM_INIT[l],
                                                            scalar2=None, op0=ALU.mult), [B_gsubt], [B_const])

            wt = [sb(ph, "wt%d" % i, [128, 3520], F32) for i in range(2)]
            wb = [sb(ph, "wb%d" % i, [128, 3520], BF16) for i in range(2)]
            B_wt = [Buf("wt%d" % i) for i in range(2)]
            B_wb = [Buf("wb%d" % i) for i in range(2)]
            jobs = []
            for l in range(nl):
                for k in range(8):
                    for h in range(2):
                        jobs.append((w_in[l, k * 128:(k + 1) * 128, h * 3520:(h + 1) * 3520],
                                     wbf_in[l, k * 128:(k + 1) * 128, h * 3520:(h + 1) * 3520], 3520))
                for k in range(16):
                    jobs.append((w_out[l, k * 128:(k + 1) * 128, :], wbf_out[l, k * 128:(k + 1) * 128, :], 1024))
            cv_eng = ["pool", "dve", "act"]
            for i, (src, dst, n) in enumerate(jobs):
                s = i % 2
                sy.dma("sp", wt[s][:, :n], src, B_wt[s], True)
                ce = cv_eng[i % 3]
                if ce == "act":
                    sy.op("act", lambda e, s=s, n=n: e.copy(out=wb[s][:, :n], in_=wt[s][:, :n]), [B_wt[s]], [B_wb[s]])
                else:
                    sy.op(ce, lambda e, s=s, n=n: e.tensor_copy(out=wb[s][:, :n], in_=wt[s][:, :n]), [B_wt[s]], [B_wb[s]])
                sy.dma("pool", dst, wb[s][:, :n], B_wb[s], False)

            wa = [sb(ph, "wa%d" % i, [128, 8, 512], F32) for i in range(2)]
            B_wa = [Buf("wa%d" % i) for i in range(2)]
            brow = sb(ph, "brow", [1, NL * 3072], F32)
            B_brow = Buf("brow")
            modsb = sb(ph, "modsb", [128, 3072], F32)
            B_modsb = Buf("modsb")
            gpb = sb(ph, "gpb", [128, D], F32)
            gqb = sb(ph, "gqb", [128, D], F32)
            a1 = sb(ph, "a1", [128, D], F32)
            gt = sb(ph, "gt", [128, D], F32)
            B_gpb, B_gqb, B_a1, B_gt = Buf("gpb"), Buf("gqb"), Buf("a1"), Buf("gt")
            pm = [ps(ph, "pm%d" % i, [128, 512], F32) for i in range(2)]
            B_pm = [Buf("pm%d" % i) for i in range(2)]
            sy.dma("sp", brow[:], b_ada.rearrange("l f -> (l f)")[None, :], B_brow, True)
            it = 0
            for l in range(nl):
                sy.dma("sp", gpb[:], g_pre[l].partition_broadcast(128), B_gpb, True)
                sy.dma("sp", gqb[:], g_post[l].partition_broadcast(128), B_gqb, True)
                for n in range(6):
                    s = it % 2
                    it += 1
                    sy.dma("sp", wa[s][:], w_ada[l, :, n * 512:(n + 1) * 512].rearrange("(k p) f -> p k f", p=128), B_wa[s], True)
                    for k in range(8):
                        sy.pe_nosig(lambda e, k=k, s=s: e.matmul(pm[s][:], lhsT=crep[:, k, :], rhs=wa[s][:, k, :],
                                                                 start=(k == 0), stop=False), [B_crep, B_wa[s]], [B_pm[s]])
                    sy.op("pe", lambda e, s=s, l=l, n=n: e.matmul(pm[s][:], lhsT=ones[0:1, :],
                                                                  rhs=brow[0:1, l * 3072 + n * 512: l * 3072 + (n + 1) * 512],
                                                                  start=False, stop=True),
                          [B_crep, B_wa[s], B_ones, B_brow], [B_pm[s]])
                    sy.op("dve", lambda e, s=s, n=n: e.tensor_copy(out=modsb[:, n * 512:(n + 1) * 512], in_=pm[s][:]),
                          [B_pm[s]], [B_modsb])
                sy.op("dve", lambda e: e.scalar_tensor_tensor(out=a1[:], in0=modsb[:, 1024:2048], scalar=1.0, in1=gpb[:],
                                                              op0=ALU.add, op1=ALU.mult), [B_modsb, B_gpb], [B_a1])
                sy.op("dve", lambda e: e.tensor_tensor(out=gt[:], in0=modsb[:, 2048:3072], in1=gqb[:], op=ALU.mult),
                      [B_modsb, B_gqb], [B_gt])
                sy.dma("pool", modb[l, 0], a1[:], B_a1, False)
                sy.dma("pool", modb[l, 1], modsb[:, 0:1024], B_modsb, False)
                sy.dma("pool", modb[l, 2], gt[:], B_gt, False)

            bt = [sb(ph, "bt%d" % i, [128, CSTRIP], F32) for i in range(2)]
            mt = [sb(ph, "mt%d" % i, [128, CSTRIP], F32) for i in range(2)]
            eb = [sb(ph, "eb%d" % i, [128, CSTRIP], BF16) for i in range(2)]
            B_bt = [Buf("bt%d" % i) for i in range(2)]
            B_mt = [Buf("mt%d" % i) for i in range(2)]
            B_eb = [Buf("eb%d" % i) for i in range(2)]
            tj = [(biasAB[:, g, :], maskAB[:, g, :], ebAB[:, g, :], 1024) for g in range(12)]
            tj += [(biasC[:, h, :], maskC, ebC[:, h, :], CSTRIP) for h in range(4)]
            negb31 = sb(ph, "negb31", [128, 4], F32)
            B_negb31 = Buf("negb31")
            sy.op("dve", lambda e: e.tensor_scalar(out=negb31[:], in0=b31[:], scalar1=-1.0, scalar2=None, op0=ALU.mult),
                  [B_b31], [B_negb31])
            for i, (bsrc, msrc, dst, n) in enumerate(tj):
                s = i % 2
                sy.dma("sp", bt[s][:, :n], bsrc, B_bt[s], True)
                sy.dma("sp", mt[s][:, :n], msrc, B_mt[s], True)
                if i < 12:
                    sy.op("act", lambda e, s=s, n=n: e.activation(out=bt[s][:, :n], in_=bt[s][:, :n], func=AF.Exp), [B_bt[s]], [B_bt[s]])
                else:
                    hcc = i - 12
                    sy.op("act", lambda e, s=s, n=n, hcc=hcc: e.activation(out=bt[s][:, :n], in_=bt[s][:, :n], func=AF.Exp,
                                                                          bias=negb31[:, hcc:hcc + 1]), [B_bt[s], B_negb31], [B_bt[s]])
                sy.op("dve", lambda e, s=s, n=n: e.tensor_tensor(out=eb[s][:, :n], in0=bt[s][:, :n], in1=mt[s][:, :n], op=ALU.mult),
                      [B_bt[s], B_mt[s]], [B_eb[s]])
                sy.dma("pool", dst, eb[s][:, :n], B_eb[s], False)
            sy.barrier()

        for l in range(nl if stop >= 1 else 0):
            x_src = x_in if l == 0 else x1
            x_dst = out if l == nl - 1 else x1

            with contextlib.ExitStack() as ph:
                W = sb(ph, "W", [128, 8, 7040], BF16)
                B_W = Buf("W")
                a1 = sb(ph, "p1a1", [128, D], F32)
                sh = sb(ph, "p1sh", [128, D], F32)
                B_a1, B_sh = Buf("p1a1"), Buf("p1sh")
                NXB = 3
                xt = [sb(ph, "xt%d" % i, [128, D], F32) for i in range(NXB)]
                B_xt = [Buf("xt%d" % i) for i in range(NXB)]
                hb = [sb(ph, "hb%d" % i, [128, D], BF16) for i in range(2)]
                B_hb = [Buf("hb%d" % i) for i in range(2)]
                hT = [sb(ph, "hT%d" % i, [128, 8, 512], BF16) for i in range(2)]
                B_hT = [Buf("hT%d" % i) for i in range(2)]
                stgT = [sb(ph, "stgT%d" % i, [128, 4, 512], BF16) for i in range(2)]
                B_stgT = [Buf("stgT%d" % i) for i in range(2)]
                stgV = [sb(ph, "stgV%d" % i, [128, NVZ], BF16) for i in range(2)]
                B_stgV = [Buf("stgV%d" % i) for i in range(2)]
                junk = sb(ph, "junk", [128, D], BF16)
                B_junk = Buf("junk")
                st4 = [sb(ph, "st4_%d" % i, [128, 4], F32) for i in range(2)]
                B_st4 = [Buf("st4_%d" % i) for i in range(2)]
                pT = [ps(ph, "pT%d" % i, [128, 8, 128], BF16) for i in range(2)]
                B_pT = [Buf("pT%d" % i) for i in range(2)]
                pm = [ps(ph, "pmm%d" % i, [128, 512], F32) for i in range(4)]
                B_pm = [Buf("pmm%d" % i) for i in range(4)]

                for k in range(8):
                    sy.dma("sp", W[:, k, :], wbf_in[l, k * 128:(k + 1) * 128, :], B_W, True)
                sy.dma("sp", a1[:], modb[l, 0], B_a1, True)
                sy.dma("sp", sh[:], modb[l, 1], B_sh, True)

                def load_x(tt):
                    sy.dma("sp", xt[tt % NXB][:], x_src[tt * 128:(tt + 1) * 128, :], B_xt[tt % NXB], True)

                load_x(0)
                load_x(1)
                pmi = 0
                evi = 0
                for st in range(NST):
                    hs = st % 2
                    for j in range(4):
                        tt = st * 4 + j
                        if tt + 2 < NT:
                            load_x(tt + 2)
                        xs, hbs, ss = tt % NXB, tt % 2, tt % 2
                        X, BX = xt[xs], B_xt[xs]
                        s4, B4 = st4[ss], B_st4[ss]
                        sy.op("act", lambda e, X=X, s4=s4: e.activation(out=junk[:], in_=X[:], func=AF.Square, accum_out=s4[:, 0:1]),
                              [BX], [B_junk, B4])
                        sy.op("dve", lambda e, s4=s4: e.tensor_scalar(out=s4[:, 1:2], in0=s4[:, 0:1], scalar1=1.0 / D, scalar2=EPS,
                                                                      op0=ALU.mult, op1=ALU.add), [B4], [B4])
                        sy.op("act", lambda e, s4=s4: e.sqrt(out=s4[:, 2:3], in_=s4[:, 1:2]), [B4], [B4])
                        sy.op("dve", lambda e, s4=s4: e.reciprocal(out=s4[:, 3:4], in_=s4[:, 2:3]), [B4], [B4])
                        sy.op("dve", lambda e, X=X, s4=s4: e.scalar_tensor_tensor(out=X[:], in0=X[:], scalar=s4[:, 3:4], in1=a1[:],
                                                                                  op0=ALU.mult, op1=ALU.mult), [BX, B4, B_a1], [BX])
                        sy.op("pool", lambda e, X=X, hbs=hbs: e.tensor_tensor(out=hb[hbs][:], in0=X[:], in1=sh[:], op=ALU.add),
                              [BX, B_sh], [B_hb[hbs]])
                        for k in range(8):
                            f = lambda e, k=k, hbs=hbs: e.transpose(out=pT[hbs][:, k, :], in_=hb[hbs][:, k * 128:(k + 1) * 128],
                                                                    identity=ident[:])
                            if k < 7:
                                sy.pe_nosig(f, [B_hb[hbs], B_const], [B_pT[hbs]])
                            else:
                                sy.op("pe", f, [B_hb[hbs], B_const], [B_pT[hbs]])
                        ee = "dve" if (tt % 2 == 0) else "act"
                        if ee == "dve":
                            sy.op("dve", lambda e, hbs=hbs, hs=hs, j=j: e.tensor_copy(out=hT[hs][:, :, j * 128:(j + 1) * 128], in_=pT[hbs][:]),
                                  [B_pT[hbs]], [B_hT[hs]])
                        else:
                            sy.op("act", lambda e, hbs=hbs, hs=hs, j=j: e.copy(out=hT[hs][:, :, j * 128:(j + 1) * 128], in_=pT[hbs][:]),
                                  [B_pT[hbs]], [B_hT[hs]])
                    for c4 in range(7):
                        g = (st * 7 + c4) % 2
                        nrows_tot = 0
                        for i in range(4):
                            c = c4 * 4 + i
                            if c * 128 >= NQK:
                                break
                            ncol = min(128, NQK - c * 128)
                            nrows_tot += ncol
                            p = pmi % 4
                            pmi += 1
                            for k in range(8):
                                f = lambda e, k=k, c=c, ncol=ncol, p=p: e.matmul(pm[p][:ncol, :], lhsT=W[:, k, c * 128:c * 128 + ncol],
                                                                                 rhs=hT[hs][:, k, :], start=(k == 0), stop=(k == 7))
                                if k < 7:
                                    sy.pe_nosig(f, [B_W, B_hT[hs]], [B_pm[p]])
                                else:
                                    sy.op("pe", f, [B_W, B_hT[hs]], [B_pm[p]])
                            evi += 1
                            if evi % 2 == 0:
                                sy.op("dve", lambda e, g=g, i=i, ncol=ncol, p=p: e.tensor_copy(out=stgT[g][:ncol, i, :], in_=pm[p][:ncol, :]),
                                      [B_pm[p]], [B_stgT[g]])
                            else:
                                sy.op("act", lambda e, g=g, i=i, ncol=ncol, p=p: e.copy(out=stgT[g][:ncol, i, :], in_=pm[p][:ncol, :]),
                                      [B_pm[p]], [B_stgT[g]])
                        r0 = c4 * 512
                        nfull = nrows_tot // 128
                        if nfull > 0:
                            sy.dma("sp", qkT[r0:r0 + nfull * 128, st * 512:(st + 1) * 512].rearrange("(i p) t -> p i t", p=128),
                                   stgT[g][:, 0:nfull, :], B_stgT[g], False)
                        rem = nrows_tot - nfull * 128
                        if rem > 0:
                            sy.dma("sp", qkT[r0 + nfull * 128:r0 + nfull * 128 + rem, st * 512:(st + 1) * 512],
                                   stgT[g][:rem, nfull, :], B_stgT[g], False)
                    for j in range(4):
                        tt = st * 4 + j
                        g = tt % 2
                        for n in range(7):
                            ncol = min(512, NVZ - n * 512)
                            p = pmi % 4
                            pmi += 1
                            for k in range(8):
                                f = lambda e, k=k, n=n, ncol=ncol, p=p, j=j: e.matmul(
                                    pm[p][:, :ncol], lhsT=hT[hs][:, k, j * 128:(j + 1) * 128],
                                    rhs=W[:, k, NQK + n * 512:NQK + n * 512 + ncol], start=(k == 0), stop=(k == 7))
                                if k < 7:
                                    sy.pe_nosig(f, [B_W, B_hT[hs]], [B_pm[p]])
                                else:
                                    sy.op("pe", f, [B_W, B_hT[hs]], [B_pm[p]])
                            evi += 1
                            if evi % 2 == 0:
                                sy.op("dve", lambda e, g=g, n=n, ncol=ncol, p=p: e.tensor_copy(out=stgV[g][:, n * 512:n * 512 + ncol], in_=pm[p][:, :ncol]),
                                      [B_pm[p]], [B_stgV[g]])
                            else:
                                sy.op("act", lambda e, g=g, n=n, ncol=ncol, p=p: e.copy(out=stgV[g][:, n * 512:n * 512 + ncol], in_=pm[p][:, :ncol]),
                                      [B_pm[p]], [B_stgV[g]])
                        sy.dma("sp", vz[tt * 128:(tt + 1) * 128, :], stgV[g][:], B_stgV[g], False)
                sy.barrier()
            if stop < 2:
                continue

            with contextlib.ExitStack() as ph:
                EB = sb(ph, "EB", [128, 12, 1024], BF16)
                B_EB = Buf("EB")
                aqT = sb(ph, "aqT", [128, 6, 2048], BF16)
                akT = sb(ph, "akT", [128, 3, 2176], BF16)
                bqT = sb(ph, "bqT", [128, 6, 2048], BF16)
                bkT = sb(ph, "bkT", [128, 6, 2, 2048], BF16)
                B_aqT, B_akT, B_bqT = Buf("aqT"), Buf("akT"), Buf("bqT")
                B_bkT = [Buf("bkT0"), Buf("bkT1")]
                NV = 4
                vaug = [sb(ph, "vaug%d" % i, [128, 12, 65], BF16) for i in range(NV)]
                B_vaug = [Buf("vaug%d" % i) for i in range(NV)]
                Et = [sb(ph, "Et%d" % i, [128, 1024], BF16) for i in range(2)]
                B_Et = [Buf("Et%d" % i) for i in range(2)]
                PTt = [sb(ph, "PTt%d" % i, [128, 1024], BF16) for i in range(3)]
                B_PTt = [Buf("PTt%d" % i) for i in range(3)]
                ostg = [sb(ph, "ostg%d" % i, [128, 780], F32) for i in range(2)]
                B_ostg = [Buf("ostg%d" % i) for i in range(2)]
                pS = [ps(ph, "pS%d" % i, [128, 1024], F32) for i in range(2)]
                B_pS = [Buf("pS%d" % i) for i in range(2)]
                pacc = [ps(ph, "pacc%d" % i, [128, 2, 512], F32) for i in range(2)]
                B_pacc = [Buf("pacc%d" % i) for i in range(2)]

                sy.dma("sp", EB[:], ebAB, B_EB, True)
                for i in range(NV):
                    sy.op("pool", lambda e, i=i: e.memset(vaug[i][:], 1.0), [], [B_vaug[i]])

                cnt = {"v": 0, "s": 0, "pt": 0, "blk": 0, "e": 0}

                items = []

                def band_block(pi, nh, G, hasprev, qT, B_q, qsel, kfun, B_k, vcol0, rows_cur, rows_prev, orows):
                    nkv = nh // G
                    tiles = [0, 1] if hasprev else [1]
                    st_ = {}
                    ngrp = nh // 4

                    def stageA(hg):
                        if hg == 0:
                            vs = {}
                            for t in tiles:
                                s_ = cnt["v"] % NV
                                cnt["v"] += 1
                                rows = rows_prev if t == 0 else rows_cur
                                sy.dma("sp", vaug[s_][:, :nkv, 0:64],
                                       vz[rows, vcol0:vcol0 + nkv * 64].rearrange("p (h e) -> p h e", e=64), B_vaug[s_], True)
                                vs[t] = s_
                            st_["vs"] = vs
                            st_["ab"] = cnt["blk"] % 2
                            cnt["blk"] += 1
                        ss = cnt["s"] % 2
                        cnt["s"] += 1
                        i = 0
                        for t in (0, 1):
                            tk = t if hasprev else 1
                            for hh in range(4):
                                h = hg * 4 + hh
                                hp = (h % 2) * 64
                                col = (h % 2) * 512 + (t * 2 + hh // 2) * 128
                                f = lambda e, tk=tk, col=col, h=h, hp=hp: e.matmul(
                                    pS[ss][:, col:col + 128], lhsT=kfun(h, tk), rhs=qT[hp:hp + 64, h // 2, qsel], start=True, stop=True)
                                i += 1
                                if i < 8:
                                    sy.pe_nosig(f, [B_q] + B_k, [B_pS[ss]])
                                else:
                                    sy.op("pe", f, [B_q] + B_k, [B_pS[ss]])
                        es = cnt["e"] % 2
                        cnt["e"] += 1
                        pt = cnt["pt"] % 3
                        cnt["pt"] += 1
                        st_[("pt", hg)] = pt
                        sy.op("act", lambda e: e.activation(out=Et[es][:], in_=pS[ss][:], func=AF.Exp, scale=0.125),
                              [B_pS[ss]], [B_Et[es]])
                        sy.op("dve", lambda e: e.tensor_tensor(out=PTt[pt][:], in0=Et[es][:], in1=EB[:, pi * 3 + hg, :], op=ALU.mult),
                              [B_Et[es], B_EB], [B_PTt[pt]])

                    def stageB(hg):
                        vs, ab, pt = st_["vs"], st_["ab"], st_[("pt", hg)]
                        for hh in range(4):
                            h = hg * 4 + hh
                            for ti, t in enumerate(tiles):
                                c0 = (h % 2) * 512 + (t * 2 + hh // 2) * 128
                                f = lambda e, t=t, h=h, c0=c0, ti=ti: e.matmul(
                                    pacc[ab][:, h // 6, (h % 6) * 65:(h % 6) * 65 + 65], lhsT=PTt[pt][:, c0:c0 + 128],
                                    rhs=vaug[vs[t]][:, h // G, :], start=(ti == 0), stop=(ti == len(tiles) - 1))
                                last = (hh == 3 and ti == len(tiles) - 1)
                                rb = [B_PTt[pt]] + [B_vaug[vs[x]] for x in tiles]
                                if not last:
                                    sy.pe_nosig(f, rb, [B_pacc[ab]])
                                else:
                                    sy.op("pe", f, rb, [B_pacc[ab]])
                        if hg != ngrp - 1:
                            return
                        nb = nh * 65
                        n0 = min(nb, 390)
                        sy.op("act", lambda e: e.copy(out=ostg[ab][:, 0:n0], in_=pacc[ab][:, 0, 0:n0]), [B_pacc[ab]], [B_ostg[ab]])
                        if nb > 390:
                            sy.op("dve", lambda e: e.tensor_copy(out=ostg[ab][:, 390:nb], in_=pacc[ab][:, 1, 0:nb - 390]),
                                  [B_pacc[ab]], [B_ostg[ab]])
                        sy.dma("sp", OB[pi, orows, 0:nb], ostg[ab][:, 0:nb], B_ostg[ab], False)

                    for hg in range(ngrp):
                        items.append(("step", (lambda hg=hg: stageA(hg)), (lambda hg=hg: stageB(hg))))

                for sp in range(NSP):
                    t0 = sp * 2048
                    slot = sp % 2

                    def span_loads(sp=sp, t0=t0, slot=slot):
                        sy.dma("sp", aqT[:], qkT[AQ0:AQ0 + 768, t0:t0 + 2048].rearrange("(c p) t -> p c t", p=128), B_aqT, True)
                        sy.dma("sp", bqT[:], qkT[BQ0:BQ0 + 768, t0:t0 + 2048].rearrange("(c p) t -> p c t", p=128), B_bqT, True)
                        sy.dma("sp", bkT[:, :, slot, :], qkT[BK0:BK0 + 768, t0:t0 + 2048].rearrange("(c p) t -> p c t", p=128),
                               B_bkT[slot], True)
                        a0 = 0 if sp > 0 else 128
                        for half in range(2):
                            sy.dma("sp", akT[half * 64:(half + 1) * 64, :, a0:2176],
                                   qkT[AK0:AK0 + 192, t0 - 128 + a0:t0 + 2048].rearrange("(g p) t -> p g t", p=64), B_akT, True)
                    items.append(("load", span_loads))
                    for i in range(16):
                        hasprev = (sp > 0 or i > 0)
                        qsel = slice(128 * i, 128 * i + 128)

                        def kfunA(h, t, i=i):
                            hp = (h % 2) * 64
                            c = 128 * i + 128 * t
                            return akT[hp:hp + 64, h // 4, c:c + 128]
                        rc = slice(t0 + 128 * i, t0 + 128 * i + 128)
                        rp = slice(t0 + 128 * i - 128, t0 + 128 * i)
                        band_block(0, 12, 4, hasprev, aqT, B_aqT, qsel, kfunA, [B_akT], AV0, rc, rp, rc)
                    for i in range(16):
                        hasprev = (sp > 0 or i > 0)
                        qsel = slice(128 * i, 128 * i + 128)

                        def kfunB1(h, t, i=i, slot=slot):
                            hp = (h % 2) * 64
                            if t == 1:
                                return bkT[hp:hp + 64, h // 2, slot, 128 * i:128 * i + 128]
                            if i > 0:
                                return bkT[hp:hp + 64, h // 2, slot, 128 * i - 128:128 * i]
                            return bkT[hp:hp + 64, h // 2, 1 - slot, 1920:2048]
                        rc = slice(t0 + 128 * i, t0 + 128 * i + 128)
                        rp = slice(t0 + 128 * i - 128, t0 + 128 * i)
                        band_block(1, 12, 1, hasprev, bqT, B_bqT, qsel, kfunB1, B_bkT, BV0, rc, rp, rc)
                    for n4 in range(4):
                        for r in range(4):
                            hasprev = (sp > 0 or n4 > 0)
                            qsel = slice(512 * n4 + r, 512 * n4 + 512, 4)

                            def kfunB4(h, t, n4=n4, r=r, slot=slot):
                                hp = (h % 2) * 64
                                if t == 1:
                                    return bkT[hp:hp + 64, h // 2, slot, 512 * n4 + r:512 * n4 + 512:4]
                                if n4 > 0:
                                    return bkT[hp:hp + 64, h // 2, slot, 512 * (n4 - 1) + r:512 * n4:4]
                                return bkT[hp:hp + 64, h // 2, 1 - slot, 1536 + r:2048:4]
                            b0 = t0 + 512 * n4 + r
                            rc = slice(b0, b0 + 509, 4)
                            rp = slice(b0 - 512, b0 - 3, 4)
                            band_block(2, 12, 1, hasprev, bqT, B_bqT, qsel, kfunB4, B_bkT, BV0, rc, rp, rc)
                    for r in range(16):
                        hasprev = (sp > 0)
                        qsel = slice(r, 2048, 16)

                        def kfunB16(h, t, r=r, slot=slot):
                            hp = (h % 2) * 64
                            sl = slot if t == 1 else 1 - slot
                            return bkT[hp:hp + 64, h // 2, sl, r:2048:16]
                        b0 = t0 + r
                        rc = slice(b0, b0 + 2033, 16)
                        rp = slice(b0 - 2048, b0 - 15, 16)
                        band_block(3, 12, 1, hasprev, bqT, B_bqT, qsel, kfunB16, B_bkT, BV0, rc, rp, rc)
                pending = None
                for it in items:
                    if it[0] == "load":
                        it[1]()
                    else:
                        it[1]()
                        if pending is not None:
                            pending()
                        pending = it[2]
                if pending is not None:
                    pending()
                sy.barrier()
            if stop < 3:
                continue

            with contextlib.ExitStack() as ph:
                strip = sb(ph, "strip", [128, 4, CSTRIP], BF16)
                B_strip = Buf("strip")
                QT = [sb(ph, "cQT%d" % i, [128, S], BF16) for i in range(2)]
                K1 = [sb(ph, "cK1%d" % i, [128, S], BF16) for i in range(2)]
                K2 = [sb(ph, "cK2%d" % i, [128, S], BF16) for i in range(2)]
                VA = [sb(ph, "cVA%d" % i, [128, NT, 129], BF16) for i in range(2)]
                B_QT = [Buf("cQT%d" % i) for i in range(2)]
                B_K1 = [Buf("cK1%d" % i) for i in range(2)]
                B_K2 = [Buf("cK2%d" % i) for i in range(2)]
                B_VA = [Buf("cVA%d" % i) for i in range(2)]
                NE, NP = 4, 4
                Ec = [sb(ph, "Ec%d" % i, [128, 512], BF16) for i in range(NE)]
                B_Ec = [Buf("Ec%d" % i) for i in range(NE)]
                Pc = [sb(ph, "Pc%d" % i, [128, 512], BF16) for i in range(NP)]
                B_Pc = [Buf("Pc%d" % i) for i in range(NP)]
                Y1 = sb(ph, "Y1", [128, 4, 128], F32)
                B_Y1 = [Buf("Y1_%d" % i) for i in range(4)]
                yd = [sb(ph, "yd%d" % i, [128, 128], F32) for i in range(2)]
                B_yd = [Buf("yd%d" % i) for i in range(2)]
                sm = [sb(ph, "sm%d" % i, [128, 8], F32) for i in range(2)]
                B_sm = [Buf("sm%d" % i) for i in range(2)]
                cjunk = sb(ph, "cjunk", [128, 128], F32)
                B_cjunk = Buf("cjunk")
                ystg = [sb(ph, "ystg%d" % i, [128, 4, 128], F32) for i in range(2)]
                B_ystg = [Buf("ystg%d" % i) for i in range(2)]
                NSB = 4
                pSc = [ps(ph, "pSc%d" % i, [128, 512], F32) for i in range(NSB)]
                B_pSc = [Buf("pSc%d" % i) for i in range(NSB)]
                pA = [ps(ph, "pA%d" % i, [128, 512], F32) for i in range(4)]
                B_pA = [Buf("pA%d" % i) for i in range(4)]

                sy.dma("sp", strip[:], ebC, B_strip, True)
                for i in range(2):
                    sy.op("pool", lambda e, i=i: e.memset(K1[i][64:128, :], 0.0), [], [B_K1[i]])
                    sy.op("pool", lambda e, i=i: e.memset(K2[i][0:64, :], 0.0), [], [B_K2[i]])
                    sy.op("pool", lambda e, i=i: e.memset(VA[i][:, :, 128:129], 1.0), [], [B_VA[i]])

                def load_head(hc):
                    b = hc % 2
                    sy.dma("sp", QT[b][:], qkT[CQ0 + hc * 128:CQ0 + (hc + 1) * 128, :], B_QT[b], True)
                    sy.dma("sp", K1[b][0:64, :], qkT[CK0 + hc * 128:CK0 + hc * 128 + 64, :], B_K1[b], True)
                    sy.dma("sp", K2[b][64:128, :], qkT[CK0 + hc * 128 + 64:CK0 + (hc + 1) * 128, :], B_K2[b], True)
                    nchunk = max(1, NT // 16)
                    tpc = NT // nchunk
                    for ci in range(nchunk):
                        sy.dma("sp", VA[b][:, ci * tpc:(ci + 1) * tpc, 0:128],
                               vz[ci * tpc * 128:(ci + 1) * tpc * 128, CV0 + hc * 128:CV0 + (hc + 1) * 128].rearrange("(t p) e -> p t e", p=128),
                               B_VA[b], True)

                accs = [sb(ph, "accs%d" % i, [128, 4, 129], F32) for i in range(2)]
                B_accs = [Buf("accs%d" % i) for i in range(2)]
                mhalf = sb(ph, "mhalf", [128, 1], F32)
                sy.op("pool", lambda e: e.memset(mhalf[:], -0.5), [], [B_const])

                load_head(0)
                steps = []
                for hc in range(4):
                    for qc in range(NST):
                        for m in range(2):
                            for kt in range(4 * qc + 4):
                                steps.append((hc, qc, m, kt))
                LOOK = 3
                cix = {"e": 0, "g": 0}
                loaded = {0}

                def stageA(i):
                    hc, qc, m, kt = steps[i]
                    b = hc % 2
                    KP, B_KP = (K1[b], B_K1[b]) if m == 0 else (K2[b], B_K2[b])
                    q0 = max(qc * 512, kt * 128)
                    nq = (qc + 1) * 512 - q0
                    dmin = q0 // 128 - kt
                    s_ = i % NSB
                    p = i % NP
                    sy.op("pe", lambda e: e.matmul(pSc[s_][:, :nq], lhsT=KP[:, kt * 128:(kt + 1) * 128], rhs=QT[b][:, q0:q0 + nq],
                                                   start=True, stop=True), [B_KP, B_QT[b]], [B_pSc[s_]])
                    if dmin >= CNEAR:
                        sy.op("act", lambda e: e.activation(out=Pc[p][:, :nq], in_=pSc[s_][:, :nq], func=AF.Exp, scale=0.125),
                              [B_pSc[s_]], [B_Pc[p]])
                    else:
                        ei = cix["e"] % NE
                        cix["e"] += 1
                        x0 = q0 - kt * 128 + 384
                        sy.op("act", lambda e: e.activation(out=Ec[ei][:, :nq], in_=pSc[s_][:, :nq], func=AF.Exp, scale=0.125),
                              [B_pSc[s_]], [B_Ec[ei]])
                        sy.op("dve", lambda e: e.tensor_tensor(out=Pc[p][:, :nq], in0=Ec[ei][:, :nq], in1=strip[:, hc, x0:x0 + nq],
                                                               op=ALU.mult), [B_Ec[ei], B_strip], [B_Pc[p]])

                def stageB(i):
                    hc, qc, m, kt = steps[i]
                    b = hc % 2
                    if (hc + 1) not in loaded and hc + 1 < 4:
                        loaded.add(hc + 1)
                        load_head(hc + 1)
                    q0 = max(qc * 512, kt * 128)
                    p = i % NP
                    qts = [qt for qt in range(4 * qc, 4 * qc + 4) if qt >= kt]
                    for qi, qt in enumerate(qts):
                        jj = qt - q0 // 128
                        a = qt % 4
                        f = lambda e, a=a, jj=jj, qt=qt: e.matmul(pA[a][:, 0:129], lhsT=Pc[p][:, jj * 128:(jj + 1) * 128],
                                                                  rhs=VA[b][:, kt, :], start=(kt == 0), stop=(kt == qt))
                        if qi < len(qts) - 1:
                            sy.pe_nosig(f, [B_Pc[p], B_VA[b]], [B_pA[a]])
                        else:
                            wl = [B_pA[x % 4] for x in qts]
                            sy.op("pe", f, [B_Pc[p], B_VA[b]], wl)
                    if kt != 4 * qc + 3:
                        return
                    g = cix["g"] % 2
                    cix["g"] += 1
                    AC, B_AC = accs[g], B_accs[g]
                    yb = (hc * NST + qc) % 2
                    for a in range(4):
                        sy.op("dve", lambda e, a=a: e.tensor_copy(out=AC[:, a, :], in_=pA[a][:, 0:129]), [B_pA[a]], [B_AC])
                    for a in range(4):
                        smb, B_smb = sm[a % 2], B_sm[a % 2]
                        sy.op("dve", lambda e, a=a, smb=smb: e.reciprocal(out=smb[:, 0:1], in_=AC[:, a, 128:129]), [B_AC], [B_smb])
                        if m == 0:
                            sy.op("dve", lambda e, a=a, smb=smb: e.tensor_scalar(out=Y1[:, a, :], in0=AC[:, a, 0:128], scalar1=smb[:, 0:1],
                                                                                 scalar2=None, op0=ALU.mult), [B_AC, B_smb], [B_Y1[a]])
                        else:
                            ydb, B_ydb = yd[a % 2], B_yd[a % 2]
                            sy.op("dve", lambda e, smb=smb: e.tensor_tensor(out=smb[:, 1:2], in0=smb[:, 0:1], in1=neglam[:, l:l + 1],
                                                                            op=ALU.mult), [B_smb, B_const], [B_smb])
                            sy.op("dve", lambda e, a=a, smb=smb, ydb=ydb: e.scalar_tensor_tensor(
                                out=ydb[:], in0=AC[:, a, 0:128], scalar=smb[:, 1:2], in1=Y1[:, a, :], op0=ALU.mult, op1=ALU.add),
                                [B_AC, B_smb, B_Y1[a]], [B_ydb])
                            sy.op("dve", lambda e, smb=smb, ydb=ydb: e.scalar_tensor_tensor(
                                out=cjunk[:], in0=ydb[:], scalar=1.0, in1=ydb[:], op0=ALU.mult, op1=ALU.mult, accum_out=smb[:, 2:3]),
                                [B_ydb], [B_cjunk, B_smb])
                            sy.op("dve", lambda e, smb=smb: e.tensor_scalar(out=smb[:, 3:4], in0=smb[:, 2:3], scalar1=1.0 / 128, scalar2=EPS,
                                                                            op0=ALU.mult, op1=ALU.add), [B_smb], [B_smb])
                            sy.op("pool", lambda e, smb=smb: e.tensor_tensor(out=smb[:, 5:6], in0=smb[:, 3:4], in1=mhalf[:], op=ALU.pow),
                                  [B_smb, B_const], [B_smb])
                            sy.op("dve", lambda e, a=a, smb=smb, ydb=ydb: e.scalar_tensor_tensor(
                                out=ystg[yb][:, a, :], in0=ydb[:], scalar=smb[:, 5:6], in1=gsubb[:, l, :], op0=ALU.mult, op1=ALU.mult),
                                [B_ydb, B_smb, B_const], [B_ystg[yb]])
                    if m == 1:
                        sy.dma("sp", YC[qc * 512:(qc + 1) * 512, hc * 128:(hc + 1) * 128].rearrange("(j p) e -> p j e", p=128),
                               ystg[yb][:], B_ystg[yb], False)

                nsteps = len(steps)
                for i in range(nsteps + LOOK):
                    if i < nsteps:
                        stageA(i)
                    if i >= LOOK:
                        stageB(i - LOOK)
                sy.barrier()
            if stop < 4:
                continue

            with contextlib.ExitStack() as ph:
                WO = sb(ph, "WO", [128, 16, D], BF16)
                B_WO = Buf("WO")
                gt = sb(ph, "p4gt", [128, D], F32)
                B_gt = Buf("p4gt")
                mh4 = sb(ph, "mh4", [128, 1], F32)
                B_mh4 = Buf("mh4")

                def dbl(name, shape, dt, n=2):
                    return [sb(ph, "%s%d" % (name, i), shape, dt) for i in range(n)], [Buf("%s%d" % (name, i)) for i in range(n)]
                oa, B_oa = dbl("oa", [128, 12, 65], F32)
                ob, B_ob = dbl("ob", [128, 3, 780], F32)
                yct, B_yct = dbl("yct", [128, 512], F32)
                zt, B_zt = dbl("zt", [128, 2048], BF16)
                xr, B_xr = dbl("xr", [128, D], F32)
                obs, B_obs = dbl("obs", [128, 12, 65], F32)
                rr, B_rr = dbl("rr", [128, 2, 12], F32)
                yy, B_yy = dbl("yy", [128, 2048], F32)
                szt, B_szt = dbl("szt", [128, 2048], F32)
                yg, B_yg = dbl("yg", [128, 2048], BF16)
                ygT, B_ygT = dbl("ygT", [128, 16, 128], BF16)
                fs, B_fs = dbl("fs", [128, 4], F32)
                tn, B_tn = dbl("tn", [128, D], F32)
                og, B_og = dbl("og", [128, D], F32)
                fj = sb(ph, "fj", [128, D], BF16)
                B_fj = Buf("fj")
                pTy = [ps(ph, "pTy%d" % i, [128, 16, 128], BF16) for i in range(2)]
                B_pTy = [Buf("pTy%d" % i) for i in range(2)]
                po = [ps(ph, "po%d" % i, [128, 1024], F32) for i in range(2)]
                B_po = [Buf("po%d" % i) for i in range(2)]

                for k in range(16):
                    sy.dma("sp", WO[:, k, :], wbf_out[l, k * 128:(k + 1) * 128, :], B_WO, True)
                sy.dma("sp", gt[:], modb[l, 2], B_gt, True)
                sy.op("pool", lambda e: e.memset(mh4[:], -0.5), [], [B_mh4])

                def L1(tt):
                    s = tt % 2
                    rows = slice(tt * 128, (tt + 1) * 128)
                    sy.dma("sp", oa[s][:].rearrange("p h e -> p (h e)"), OB[0, rows, :], B_oa[s], True)
                    sy.dma("sp", ob[s][:], OB[1:4, rows, :].rearrange("c p f -> p c f"), B_ob[s], True)
                    sy.dma("sp", yct[s][:], YC[rows, :], B_yct[s], True)
                    sy.dma("sp", zt[s][:], vz[rows, Z0:Z0 + 2048], B_zt[s], True)

                def L3(tt):
                    s = tt % 2
                    sy.dma("sp", xr[s][:], x_src[tt * 128:(tt + 1) * 128, :], B_xr[s], True)

                def S1(tt):
                    s = tt % 2
                    obv = lambda c: ob[s][:, c, :].rearrange("p (h e) -> p h e", e=65)
                    sy.op("act", lambda e: e.activation(out=szt[s][:], in_=zt[s][:], func=AF.Silu), [B_zt[s]], [B_szt[s]])
                    sy.op("pool", lambda e: e.tensor_tensor(out=obs[s][:], in0=obv(0), in1=obv(1), op=ALU.add), [B_ob[s]], [B_obs[s]])
                    sy.op("pool", lambda e: e.tensor_tensor(out=obs[s][:], in0=obs[s][:], in1=obv(2), op=ALU.add), [B_ob[s], B_obs[s]], [B_obs[s]])
                    sy.op("dve", lambda e: e.tensor_tensor(out=rr[s][:, 0, :], in0=oa[s][:, :, 64], in1=esink[:, l, :], op=ALU.add),
                          [B_oa[s], B_const], [B_rr[s]])
                    sy.op("dve", lambda e: e.reciprocal(out=rr[s][:, 0, :], in_=rr[s][:, 0, :]), [B_rr[s]], [B_rr[s]])
                    sy.op("dve", lambda e: e.tensor_tensor(out=yy[s][:, 0:768].rearrange("p (h e) -> p h e", e=64), in0=oa[s][:, :, 0:64],
                                                           in1=rr[s][:, 0, :].unsqueeze(2).to_broadcast([128, 12, 64]), op=ALU.mult),
                          [B_oa[s], B_rr[s]], [B_yy[s]])
                    sy.op("dve", lambda e: e.tensor_tensor(out=yg[s][:, 0:768], in0=yy[s][:, 0:768], in1=szt[s][:, 0:768], op=ALU.mult),
                          [B_yy[s], B_szt[s]], [B_yg[s]])
                    sy.op("dve", lambda e: e.reciprocal(out=rr[s][:, 1, :], in_=obs[s][:, :, 64]), [B_obs[s]], [B_rr[s]])
                    sy.op("pool", lambda e: e.tensor_tensor(out=yy[s][:, 768:1536].rearrange("p (h e) -> p h e", e=64), in0=obs[s][:, :, 0:64],
                                                            in1=rr[s][:, 1, :].unsqueeze(2).to_broadcast([128, 12, 64]), op=ALU.mult),
                          [B_obs[s], B_rr[s]], [B_yy[s]])
                    sy.op("dve", lambda e: e.tensor_tensor(out=yg[s][:, 768:1536], in0=yy[s][:, 768:1536], in1=szt[s][:, 768:1536], op=ALU.mult),
                          [B_yy[s], B_szt[s]], [B_yg[s]])
                    sy.op("pool", lambda e: e.tensor_tensor(out=yg[s][:, 1536:2048], in0=yct[s][:], in1=szt[s][:, 1536:2048], op=ALU.mult),
                          [B_yct[s], B_szt[s]], [B_yg[s]])

                def T2(tt):
                    s = tt % 2
                    for k in range(16):
                        f = lambda e, k=k: e.transpose(out=pTy[s][:, k, :], in_=yg[s][:, k * 128:(k + 1) * 128], identity=ident[:])
                        if k % 8 < 7:
                            sy.pe_nosig(f, [B_yg[s], B_const], [B_pTy[s]])
                        else:
                            sy.op("pe", f, [B_yg[s], B_const], [B_pTy[s]])
                    sy.op("act", lambda e: e.copy(out=ygT[s][:, 0:8, :], in_=pTy[s][:, 0:8, :]), [B_pTy[s]], [B_ygT[s]])
                    sy.op("act", lambda e: e.copy(out=ygT[s][:, 8:16, :], in_=pTy[s][:, 8:16, :]), [B_pTy[s]], [B_ygT[s]])

                def M2(tt):
                    s = tt % 2
                    for n in range(2):
                        for k in range(16):
                            f = lambda e, k=k, n=n: e.matmul(po[s][:, n * 512:(n + 1) * 512], lhsT=ygT[s][:, k, :],
                                                             rhs=WO[:, k, n * 512:(n + 1) * 512], start=(k == 0), stop=(k == 15))
                            if k < 15 or n == 0:
                                sy.pe_nosig(f, [B_ygT[s], B_WO], [B_po[s]])
                            else:
                                sy.op("pe", f, [B_ygT[s], B_WO], [B_po[s]])

                def S3(tt):
                    s = tt % 2
                    f4, B4 = fs[s], B_fs[s]
                    sy.op("act", lambda e: e.activation(out=fj[:], in_=po[s][:], func=AF.Square, accum_out=f4[:, 0:1]), [B_po[s]], [B_fj, B4])
                    sy.op("dve", lambda e: e.tensor_scalar(out=f4[:, 1:2], in0=f4[:, 0:1], scalar1=1.0 / D, scalar2=EPS, op0=ALU.mult, op1=ALU.add),
                          [B4], [B4])
                    sy.op("pool", lambda e: e.tensor_tensor(out=f4[:, 3:4], in0=f4[:, 1:2], in1=mh4[:], op=ALU.pow), [B4, B_mh4], [B4])
                    sy.op("dve", lambda e: e.scalar_tensor_tensor(out=tn[s][:], in0=po[s][:], scalar=f4[:, 3:4], in1=gt[:], op0=ALU.mult, op1=ALU.mult),
                          [B_po[s], B4, B_gt], [B_tn[s]])
                    sy.op("pool", lambda e: e.tensor_tensor(out=og[s][:], in0=tn[s][:], in1=xr[s][:], op=ALU.add), [B_tn[s], B_xr[s]], [B_og[s]])
                    sy.dma("sp", x_dst[tt * 128:(tt + 1) * 128, :], og[s][:], B_og[s], False)

                ok = lambda t: 0 <= t < NT
                for i in range(-3, NT + 1):
                    if ok(i + 3):
                        L1(i + 3)
                    if ok(i):
                        L3(i)
                    if ok(i + 2):
                        S1(i + 2)
                    if ok(i + 1):
                        T2(i + 1)
                    if ok(i):
                        M2(i)
                    if ok(i - 1):
                        S3(i - 1)
                sy.barrier()
        print("build done: sems=%d counts=%s" % (sy.nsem, sy.cnt))
    return nc


def _rel_bucket(dist):
    n = np.maximum(dist, 0).astype(np.int64)
    nf = np.maximum(n, 1).astype(np.float32)
    large = 16 + (np.log(nf / np.float32(16.0)) / np.float32(math.log(2048 / 16)) * np.float32(16.0)).astype(np.int32)
    large = np.minimum(large, 31)
    return np.where(n < 16, n, large).astype(np.int64)


def _host_tables(rel_table):
    rel_table = np.asarray(rel_table, dtype=np.float32)
    k = np.arange(128)[:, None]
    q = np.arange(128)[None, :]
    biasAB = np.zeros((128, 12, 2, 2, 2, 128), np.float32)
    maskAB = np.zeros((128, 12, 2, 2, 2, 128), np.float32)
    pats = [(1, 127, 0), (1, 128, 12), (4, 128, 12), (16, 128, 12)]
    for pi, (d, maxd, ho) in enumerate(pats):
        for t in range(2):
            dist = q - k + (128 if t == 0 else 0)
            idx = _rel_bucket(dist * d)
            msk = ((dist >= 0) & (dist <= maxd)).astype(np.float32)
            for hg in range(3):
                for hh in range(4):
                    biasAB[:, pi * 3 + hg, hh % 2, t, hh // 2, :] = rel_table[idx, ho + hg * 4 + hh]
                    maskAB[:, pi * 3 + hg, hh % 2, t, hh // 2, :] = msk
    biasAB = biasAB.reshape(128, 12, 1024)
    maskAB = maskAB.reshape(128, 12, 1024)
    xx = np.arange(CSTRIP)[None, :]
    dist = xx - 384 - k
    idx = _rel_bucket(dist)
    biasC = np.stack([rel_table[idx, 24 + h] for h in range(4)], axis=1).astype(np.float32)
    maskC = (dist >= 0).astype(np.float32)
    b31 = rel_table[31, 24:28].reshape(1, 4).astype(np.float32)
    return biasAB, maskAB, np.ascontiguousarray(biasC), maskC, b31


def _perm_cols():
    sizes = [768, 192, 192, 768, 768, 768, 512, 512, 512, 2048]
    offs = np.cumsum([0] + sizes)
    seg = lambda i: np.arange(offs[i], offs[i + 1])
    order = [0, 1, 3, 4, 6, 7, 2, 5, 8, 9]
    return np.concatenate([seg(i) for i in order])


_NC_CACHE = {}
ACTIVE = {0: 0, 1: 1, 4: 2, 5: 3}


def make_in_maps(inputs, S=8192):
    x = np.asarray(inputs["x"], np.float32)
    c = np.asarray(inputs["c"], np.float32)
    perm = _perm_cols()
    w_in_p = np.ascontiguousarray(np.asarray(inputs["w_in"], np.float32)[:, :, perm])
    biasAB, maskAB, biasC, maskC, b31 = _host_tables(inputs["rel_table"])
    lam4 = np.stack([np.asarray(inputs[k], np.float32) for k in ("lam_q1", "lam_k1", "lam_q2", "lam_k2")], axis=1).reshape(NL, 256)
    shared = {
        "w_in": w_in_p,
        "w_out": np.ascontiguousarray(np.asarray(inputs["w_out"], np.float32)),
        "w_ada": np.ascontiguousarray(np.asarray(inputs["w_ada"], np.float32)),
        "b_ada": np.ascontiguousarray(np.asarray(inputs["b_ada"], np.float32)),
        "g_pre": np.ascontiguousarray(np.asarray(inputs["g_pre"], np.float32)),
        "g_post": np.ascontiguousarray(np.asarray(inputs["g_post"], np.float32)),
        "a_sinks": np.ascontiguousarray(np.asarray(inputs["a_sinks"], np.float32)),
        "lam4": np.ascontiguousarray(lam4),
        "g_sub": np.ascontiguousarray(np.asarray(inputs["g_sub"], np.float32)),
        "biasAB": biasAB, "maskAB": maskAB, "biasC": biasC, "maskC": maskC, "b31": b31,
        "ident": np.eye(128, dtype=np.float32),
    }
    in_maps = []
    zero_shared = None
    for core in range(8):
        b = ACTIVE.get(core)
        if b is None:
            if zero_shared is None:
                zero_shared = {k: np.zeros_like(v) for k, v in shared.items()}
                zero_shared["xb"] = np.zeros((S, D), np.float32)
                zero_shared["c2"] = np.zeros((128, 8), np.float32)
            in_maps.append(zero_shared)
            continue
        m = dict(shared)
        m["xb"] = np.ascontiguousarray(x[b, :S])
        m["c2"] = np.ascontiguousarray(c[b].reshape(8, 128).T)
        in_maps.append(m)
    return in_maps


def kernel(x, c, rel_table, w_in, w_out, w_ada, b_ada, g_pre, g_post, a_sinks,
           lam_q1, lam_k1, lam_q2, lam_k2, g_sub):
    inputs = dict(x=x, c=c, rel_table=rel_table, w_in=w_in, w_out=w_out, w_ada=w_ada, b_ada=b_ada, g_pre=g_pre,
                  g_post=g_post, a_sinks=a_sinks, lam_q1=lam_q1, lam_k1=lam_k1, lam_q2=lam_q2, lam_k2=lam_k2, g_sub=g_sub)
    if "nc" not in _NC_CACHE:
        _NC_CACHE["nc"] = build()
    nc = _NC_CACHE["nc"]
    in_maps = make_in_maps(inputs)
    res = run_bass_kernel_spmd(nc, in_maps, core_ids=list(range(8)))
    core_of = {b: core for core, b in ACTIVE.items()}
    outs = [np.asarray(res.results[core_of[b]]["out"], np.float32) for b in range(4)]
    return np.stack(outs, axis=0)
```

```python
import math
import contextlib
import numpy as np
import concourse.bass as bass
import concourse.mybir as mybir
from concourse.bass_utils import run_bass_kernel_spmd

F32 = mybir.dt.float32
BF16 = mybir.dt.bfloat16
AF = mybir.ActivationFunctionType
ALU = mybir.AluOpType
AX = mybir.AxisListType

D = 1024
NL = 2
EPS = 1e-6
NQK = 3520
NVZ = 3520
AQ0, AK0, BQ0, BK0, CQ0, CK0 = 0, 768, 960, 1728, 2496, 3008
AV0, BV0, CV0, Z0 = 0, 192, 960, 1472
CSTRIP = 2560
CNEAR = 13
LAM_INIT = [0.8 - 0.6 * math.exp(-0.3 * l) for l in range(NL)]


class Buf:
    __slots__ = ("name", "w", "r", "dsem", "dcnt")

    def __init__(self, name):
        self.name = name
        self.w = []
        self.r = []
        self.dsem = None
        self.dcnt = 0


class Sync:
    def __init__(self, nc, stack):
        self.nc = nc
        self.stack = stack
        self.eng = {"pe": nc.tensor, "act": nc.scalar, "dve": nc.vector, "pool": nc.gpsimd, "sp": nc.sync}
        self.sem = {}
        for k in ("pe", "act", "dve", "pool"):
            self.sem[k] = stack.enter_context(nc.semaphore("e_" + k))
        self.cnt = {k: 0 for k in self.sem}
        self.seen = {k: {} for k in self.eng}
        self.dbufs = []
        self.dcount = {}
        self.nsem = 4

    def _wait(self, e, ev):
        key, val = ev
        if key == "pe" and e == "pe":
            return
        if self.seen[e].get(key, 0) >= val:
            return
        self.eng[e].wait_ge(self.sem[key], val)
        self.seen[e][key] = val

    def _deps(self, e, reads, writes):
        for b in reads:
            for ev in b.w:
                self._wait(e, ev)
        for b in writes:
            for ev in b.w:
                self._wait(e, ev)
            for ev in b.r:
                self._wait(e, ev)

    def op(self, e, fn, reads=(), writes=()):
        self._deps(e, reads, writes)
        ins = fn(self.eng[e])
        self.cnt[e] += 1
        ins.then_inc(self.sem[e], 1)
        ev = (e, self.cnt[e])
        for b in reads:
            b.r = [x for x in b.r if x[0] != e] + [ev]
        for b in writes:
            b.w = [x for x in b.w if x[0] != e] + [ev]
            b.r = []
        return ins

    def pe_nosig(self, fn, reads=(), writes=()):
        self._deps("pe", reads, writes)
        return fn(self.eng["pe"])

    def _dsem(self, b):
        if b.dsem is None:
            key = "d_" + b.name
            if key not in self.sem:
                self.sem[key] = self.stack.enter_context(self.nc.semaphore(key))
                self.dcount[key] = 0
                self.nsem += 1
            b.dsem = key
            b.dcnt = self.dcount[key]
            self.dbufs.append(b)
        return b.dsem

    def dma(self, q, out_ap, in_ap, buf, load, extra_reads=(), **kw):
        key = self._dsem(buf)
        if load:
            self._deps(q, extra_reads, [buf])
        else:
            self._deps(q, [buf] + list(extra_reads), [])
        ins = self.eng[q].dma_start(out=out_ap, in_=in_ap, **kw)
        buf.dcnt += 16
        self.dcount[key] = buf.dcnt
        ins.then_inc(self.sem[key], 16)
        ev = (key, buf.dcnt)
        if load:
            buf.w = [x for x in buf.w if x[0] != key] + [ev]
            buf.r = []
        else:
            buf.r = [x for x in buf.r if x[0] != key] + [ev]
        return ins

    def barrier(self):
        evs = [(k, self.cnt[k]) for k in ("pe", "act", "dve", "pool") if self.cnt[k] > 0]
        evs += [(k, v) for k, v in self.dcount.items() if v > 0]
        for e in self.eng:
            for ev in evs:
                if ev[0] == e:
                    continue
                self._wait(e, ev)
        for b in self.dbufs:
            b.w = []
            b.r = []
        self.dbufs = []


def build(S=8192, nl=NL, dbg=False, stop=99):
    NT = S // 128
    NST = S // 512
    NSP = S // 2048
    nc = bass.Bass("TRN2", target_bir_lowering=False)

    def din(name, shape):
        return nc.dram_tensor(name, shape, F32, kind="ExternalInput").ap()

    x_in = din("xb", [S, D])
    c_in = din("c2", [128, 8])
    w_in = din("w_in", [NL, D, 7040])
    w_out = din("w_out", [NL, 2048, D])
    w_ada = din("w_ada", [NL, D, 3072])
    b_ada = din("b_ada", [NL, 3072])
    g_pre = din("g_pre", [NL, D])
    g_post = din("g_post", [NL, D])
    a_sinks = din("a_sinks", [NL, 12])
    lam4 = din("lam4", [NL, 256])
    g_sub = din("g_sub", [NL, 128])
    biasAB = din("biasAB", [128, 12, 1024])
    maskAB = din("maskAB", [128, 12, 1024])
    biasC = din("biasC", [128, 4, CSTRIP])
    maskC = din("maskC", [128, CSTRIP])
    b31_in = din("b31", [1, 4])
    ident_in = din("ident", [128, 128])
    out = nc.dram_tensor("out", [S, D], F32, kind="ExternalOutput").ap()

    skind = "ExternalOutput" if dbg else "Internal"

    def scratch(name, shape, dt):
        return nc.dram_tensor(name, shape, dt, kind=skind).ap()

    wbf_in = scratch("wbf_in", [NL, D, 7040], BF16)
    wbf_out = scratch("wbf_out", [NL, 2048, D], BF16)
    modb = scratch("modb", [NL, 3, 128, D], F32)
    ebAB = scratch("ebAB", [128, 12, 1024], BF16)
    ebC = scratch("ebC", [128, 4, CSTRIP], BF16)
    qkT = scratch("qkT", [NQK, S], BF16)
    vz = scratch("vz", [S, NVZ], BF16)
    OB = scratch("OB", [4, S, 780], F32)
    YC = scratch("YC", [S, 512], F32)
    x1 = scratch("x1", [S, D], F32)

    with contextlib.ExitStack() as top:
        sy = Sync(nc, top)

        uid = [0]

        def sb(stack, name, shape, dt):
            uid[0] += 1
            return stack.enter_context(nc.sbuf_tensor("s%d_%s" % (uid[0], name), shape, dt))

        def ps(stack, name, shape, dt):
            uid[0] += 1
            return stack.enter_context(nc.psum_tensor("p%d_%s" % (uid[0], name), shape, dt))

        ident = sb(top, "ident", [128, 128], BF16)
        neglam = sb(top, "neglam", [128, NL], F32)
        esink = sb(top, "esink", [128, NL, 12], F32)
        gsubb = sb(top, "gsubb", [128, NL, 128], F32)
        b31 = sb(top, "b31", [128, 4], F32)
        B_const = Buf("const")

        with contextlib.ExitStack() as ph:
            identf = sb(ph, "identf", [128, 128], F32)
            ones = sb(ph, "ones", [128, 128], F32)
            cin = sb(ph, "cin", [128, 8], F32)
            cact = sb(ph, "cact", [128, 8], F32)
            crep = sb(ph, "crep", [128, 8, 128], F32)
            lamt = sb(ph, "lamt", [128, NL, 256], F32)
            lprod = sb(ph, "lprod", [128, 2, 64], F32)
            lsum = sb(ph, "lsum", [128, 2], F32)
            lexp = sb(ph, "lexp", [128, 2], F32)
            ldiff = sb(ph, "ldiff", [128, 1], F32)
            sinkt = sb(ph, "sinkt", [128, NL, 12], F32)
            gsubt = sb(ph, "gsubt", [128, NL, 128], F32)
            B_identf, B_ones, B_cin, B_cact, B_crep = Buf("identf"), Buf("ones"), Buf("cin"), Buf("cact"), Buf("crep")
            B_lamt, B_lprod, B_lsum, B_lexp, B_ldiff = Buf("lamt"), Buf("lprod"), Buf("lsum"), Buf("lexp"), Buf("ldiff")
            B_sinkt, B_gsubt, B_b31 = Buf("sinkt"), Buf("gsubt"), Buf("b31")

            sy.dma("sp", identf[:], ident_in, B_identf, True)
            sy.op("dve", lambda e: e.tensor_copy(out=ident[:], in_=identf[:]), [B_identf], [B_const])
            sy.op("pool", lambda e: e.memset(ones[:], 1.0), [], [B_ones])
            sy.dma("sp", cin[:], c_in, B_cin, True)
            sy.op("act", lambda e: e.activation(out=cact[:], in_=cin[:], func=AF.Silu), [B_cin], [B_cact])
            for k in range(8):
                sy.op("dve", lambda e, k=k: e.tensor_scalar(out=crep[:, k, :], in0=ones[:], scalar1=cact[:, k:k + 1],
                                                            scalar2=None, op0=ALU.mult), [B_ones, B_cact], [B_crep])
            sy.dma("sp", b31[:], b31_in[0].partition_broadcast(128), B_b31, True)
            sy.dma("sp", lamt[:].rearrange("p l f -> p (l f)"),
                   lam4.rearrange("l f -> (l f)").partition_broadcast(128), B_lamt, True)
            sy.dma("sp", sinkt[:].rearrange("p l f -> p (l f)"),
                   a_sinks.rearrange("l f -> (l f)").partition_broadcast(128), B_sinkt, True)
            sy.dma("sp", gsubt[:].rearrange("p l f -> p (l f)"),
                   g_sub.rearrange("l f -> (l f)").partition_broadcast(128), B_gsubt, True)
            sy.op("act", lambda e: e.activation(out=esink[:].rearrange("p l f -> p (l f)"),
                                                in_=sinkt[:].rearrange("p l f -> p (l f)"), func=AF.Exp),
                  [B_sinkt], [B_const])
            for l in range(NL):
                lt = lamt[:, l, :].rearrange("p (a f) -> p a f", a=4)
                sy.op("dve", lambda e, lt=lt: e.tensor_tensor(out=lprod[:, 0, :], in0=lt[:, 0, :], in1=lt[:, 1, :], op=ALU.mult),
                      [B_lamt], [B_lprod])
                sy.op("dve", lambda e, lt=lt: e.tensor_tensor(out=lprod[:, 1, :], in0=lt[:, 2, :], in1=lt[:, 3, :], op=ALU.mult),
                      [B_lamt], [B_lprod])
                sy.op("dve", lambda e: e.reduce_sum(out=lsum[:], in_=lprod[:], axis=AX.X), [B_lprod], [B_lsum])
                sy.op("act", lambda e: e.activation(out=lexp[:], in_=lsum[:], func=AF.Exp), [B_lsum], [B_lexp])
                sy.op("dve", lambda e: e.tensor_tensor(out=ldiff[:], in0=lexp[:, 1:2], in1=lexp[:, 0:1], op=ALU.subtract),
                      [B_lexp], [B_ldiff])
                sy.op("dve", lambda e, l=l: e.tensor_scalar(out=neglam[:, l:l + 1], in0=ldiff[:], scalar1=-LAM_INIT[l],
                                                            scalar2=None, op0=ALU.add), [B_ldiff], [B_const])
                sy.op("dve", lambda e, l=l: e.tensor_scalar(out=gsubb[:, l, :], in0=gsubt[:, l, :], scalar1=1.0 - LAM_INIT[l],
                                                            scalar2=None, op0=ALU.mult), [B_gsubt], [B_const])

            wt = [sb(ph, "wt%d" % i, [128, 3520], F32) for i in range(2)]
            wb = [sb(ph, "wb%d" % i, [128, 3520], BF16) for i in range(2)]
            B_wt = [Buf("wt%d" % i) for i in range(2)]
            B_wb = [Buf("wb%d" % i) for i in range(2)]
            jobs = []
            for l in range(nl):
                for k in range(8):
                    for h in range(2):
                        jobs.append((w_in[l, k * 128:(k + 1) * 128, h * 3520:(h + 1) * 3520],
                                     wbf_in[l, k * 128:(k + 1) * 128, h * 3520:(h + 1) * 3520], 3520))
                for k in range(16):
                    jobs.append((w_out[l, k * 128:(k + 1) * 128, :], wbf_out[l, k * 128:(k + 1) * 128, :], 1024))
            cv_eng = ["pool", "dve", "act"]
            for i, (src, dst, n) in enumerate(jobs):
                s = i % 2
                sy.dma("sp", wt[s][:, :n], src, B_wt[s], True)
                ce = cv_eng[i % 3]
                if ce == "act":
                    sy.op("act", lambda e, s=s, n=n: e.copy(out=wb[s][:, :n], in_=wt[s][:, :n]), [B_wt[s]], [B_wb[s]])
                else:
                    sy.op(ce, lambda e, s=s, n=n: e.tensor_copy(out=wb[s][:, :n], in_=wt[s][:, :n]), [B_wt[s]], [B_wb[s]])
                sy.dma("pool", dst, wb[s][:, :n], B_wb[s], False)

            wa = [sb(ph, "wa%d" % i, [128, 8, 512], F32) for i in range(2)]
            B_wa = [Buf("wa%d" % i) for i in range(2)]
            brow = sb(ph, "brow", [1, NL * 3072], F32)
            B_brow = Buf("brow")
            modsb = sb(ph, "modsb", [128, 3072], F32)
            B_modsb = Buf("modsb")
            gpb = sb(ph, "gpb", [128, D], F32)
            gqb = sb(ph, "gqb", [128, D], F32)
            a1 = sb(ph, "a1", [128, D], F32)
            gt = sb(ph, "gt", [128, D], F32)
            B_gpb, B_gqb, B_a1, B_gt = Buf("gpb"), Buf("gqb"), Buf("a1"), Buf("gt")
            pm = [ps(ph, "pm%d" % i, [128, 512], F32) for i in range(2)]
            B_pm = [Buf("pm%d" % i) for i in range(2)]
            sy.dma("sp", brow[:], b_ada.rearrange("l f -> (l f)")[None, :], B_brow, True)
            it = 0
            for l in range(nl):
                sy.dma("sp", gpb[:], g_pre[l].partition_broadcast(128), B_gpb, True)
                sy.dma("sp", gqb[:], g_post[l].partition_broadcast(128), B_gqb, True)
                for n in range(6):
                    s = it % 2
                    it += 1
                    sy.dma("sp", wa[s][:], w_ada[l, :, n * 512:(n + 1) * 512].rearrange("(k p) f -> p k f", p=128), B_wa[s], True)
                    for k in range(8):
                        sy.pe_nosig(lambda e, k=k, s=s: e.matmul(pm[s][:], lhsT=crep[:, k, :], rhs=wa[s][:, k, :],
                                                                 start=(k == 0), stop=False), [B_crep, B_wa[s]], [B_pm[s]])
                    sy.op("pe", lambda e, s=s, l=l, n=n: e.matmul(pm[s][:], lhsT=ones[0:1, :],
                                                                  rhs=brow[0:1, l * 3072 + n * 512: l * 3072 + (n + 1) * 512],
                                                                  start=False, stop=True),
                          [B_crep, B_wa[s], B_ones, B_brow], [B_pm[s]])
                    sy.op("dve", lambda e, s=s, n=n: e.tensor_copy(out=modsb[:, n * 512:(n + 1) * 512], in_=pm[s][:]),
                          [B_pm[s]], [B_modsb])
                sy.op("dve", lambda e: e.scalar_tensor_tensor(out=a1[:], in0=modsb[:, 1024:2048], scalar=1.0, in1=gpb[:],
                                                              op0=ALU.add, op1=ALU.mult), [B_modsb, B_gpb], [B_a1])
                sy.op("dve", lambda e: e.tensor_tensor(out=gt[:], in0=modsb[:, 2048:3072], in1=gqb[:], op=ALU.mult),
                      [B_modsb, B_gqb], [B_gt])
                sy.dma("pool", modb[l, 0], a1[:], B_a1, False)
                sy.dma("pool", modb[l, 1], modsb[:, 0:1024], B_modsb, False)
                sy.dma("pool", modb[l, 2], gt[:], B_gt, False)

            bt = [sb(ph, "bt%d" % i, [128, CSTRIP], F32) for i in range(2)]
            mt = [sb(ph, "mt%d" % i, [128, CSTRIP], F32) for i in range(2)]
            eb = [sb(ph, "eb%d" % i, [128, CSTRIP], BF16) for i in range(2)]
            B_bt = [Buf("bt%d" % i) for i in range(2)]
            B_mt = [Buf("mt%d" % i) for i in range(2)]
            B_eb = [Buf("eb%d" % i) for i in range(2)]
            tj = [(biasAB[:, g, :], maskAB[:, g, :], ebAB[:, g, :], 1024) for g in range(12)]
            tj += [(biasC[:, h, :], maskC, ebC[:, h, :], CSTRIP) for h in range(4)]
            negb31 = sb(ph, "negb31", [128, 4], F32)
            B_negb31 = Buf("negb31")
            sy.op("dve", lambda e: e.tensor_scalar(out=negb31[:], in0=b31[:], scalar1=-1.0, scalar2=None, op0=ALU.mult),
                  [B_b31], [B_negb31])
            for i, (bsrc, msrc, dst, n) in enumerate(tj):
                s = i % 2
                sy.dma("sp", bt[s][:, :n], bsrc, B_bt[s], True)
                sy.dma("sp", mt[s][:, :n], msrc, B_mt[s], True)
                if i < 12:
                    sy.op("act", lambda e, s=s, n=n: e.activation(out=bt[s][:, :n], in_=bt[s][:, :n], func=AF.Exp), [B_bt[s]], [B_bt[s]])
                else:
                    hcc = i - 12
                    sy.op("act", lambda e, s=s, n=n, hcc=hcc: e.activation(out=bt[s][:, :n], in_=bt[s][:, :n], func=AF.Exp,
                                                                          bias=negb31[:, hcc:hcc + 1]), [B_bt[s], B_negb31], [B_bt[s]])
                sy.op("dve", lambda e, s=s, n=n: e.tensor_tensor(out=eb[s][:, :n], in0=bt[s][:, :n], in1=mt[s][:, :n], op=ALU.mult),
                      [B_bt[s], B_mt[s]], [B_eb[s]])
                sy.dma("pool", dst, eb[s][:, :n], B_eb[s], False)
            sy.barrier()

        for l in range(nl if stop >= 1 else 0):
            x_src = x_in if l == 0 else x1
            x_dst = out if l == nl - 1 else x1

            with contextlib.ExitStack() as ph:
                W = sb(ph, "W", [128, 8, 7040], BF16)
                B_W = Buf("W")
                a1 = sb(ph, "p1a1", [128, D], F32)
                sh = sb(ph, "p1sh", [128, D], F32)
                B_a1, B_sh = Buf("p1a1"), Buf("p1sh")
                NXB = 3
                xt = [sb(ph, "xt%d" % i, [128, D], F32) for i in range(NXB)]
                B_xt = [Buf("xt%d" % i) for i in range(NXB)]
                hb = [sb(ph, "hb%d" % i, [128, D], BF16) for i in range(2)]
                B_hb = [Buf("hb%d" % i) for i in range(2)]
                hT = [sb(ph, "hT%d" % i, [128, 8, 512], BF16) for i in range(2)]
                B_hT = [Buf("hT%d" % i) for i in range(2)]
                stgT = [sb(ph, "stgT%d" % i, [128, 4, 512], BF16) for i in range(2)]
                B_stgT = [Buf("stgT%d" % i) for i in range(2)]
                stgV = [sb(ph, "stgV%d" % i, [128, NVZ], BF16) for i in range(2)]
                B_stgV = [Buf("stgV%d" % i) for i in range(2)]
                junk = sb(ph, "junk", [128, D], BF16)
                B_junk = Buf("junk")
                st4 = [sb(ph, "st4_%d" % i, [128, 4], F32) for i in range(2)]
                B_st4 = [Buf("st4_%d" % i) for i in range(2)]
                pT = [ps(ph, "pT%d" % i, [128, 8, 128], BF16) for i in range(2)]
                B_pT = [Buf("pT%d" % i) for i in range(2)]
                pm = [ps(ph, "pmm%d" % i, [128, 512], F32) for i in range(4)]
                B_pm = [Buf("pmm%d" % i) for i in range(4)]

                for k in range(8):
                    sy.dma("sp", W[:, k, :], wbf_in[l, k * 128:(k + 1) * 128, :], B_W, True)
                sy.dma("sp", a1[:], modb[l, 0], B_a1, True)
                sy.dma("sp", sh[:], modb[l, 1], B_sh, True)

                def load_x(tt):
                    sy.dma("sp", xt[tt % NXB][:], x_src[tt * 128:(tt + 1) * 128, :], B_xt[tt % NXB], True)

                load_x(0)
                load_x(1)
                pmi = 0
                evi = 0
                for st in range(NST):
                    hs = st % 2
                    for j in range(4):
                        tt = st * 4 + j
                        if tt + 2 < NT:
                            load_x(tt + 2)
                        xs, hbs, ss = tt % NXB, tt % 2, tt % 2
                        X, BX = xt[xs], B_xt[xs]
                        s4, B4 = st4[ss], B_st4[ss]
                        sy.op("act", lambda e, X=X, s4=s4: e.activation(out=junk[:], in_=X[:], func=AF.Square, accum_out=s4[:, 0:1]),
                              [BX], [B_junk, B4])
                        sy.op("dve", lambda e, s4=s4: e.tensor_scalar(out=s4[:, 1:2], in0=s4[:, 0:1], scalar1=1.0 / D, scalar2=EPS,
                                                                      op0=ALU.mult, op1=ALU.add), [B4], [B4])
                        sy.op("act", lambda e, s4=s4: e.sqrt(out=s4[:, 2:3], in_=s4[:, 1:2]), [B4], [B4])
                        sy.op("dve", lambda e, s4=s4: e.reciprocal(out=s4[:, 3:4], in_=s4[:, 2:3]), [B4], [B4])
                        sy.op("dve", lambda e, X=X, s4=s4: e.scalar_tensor_tensor(out=X[:], in0=X[:], scalar=s4[:, 3:4], in1=a1[:],
                                                                                  op0=ALU.mult, op1=ALU.mult), [BX, B4, B_a1], [BX])
                        sy.op("pool", lambda e, X=X, hbs=hbs: e.tensor_tensor(out=hb[hbs][:], in0=X[:], in1=sh[:], op=ALU.add),
                              [BX, B_sh], [B_hb[hbs]])
                        for k in range(8):
                            f = lambda e, k=k, hbs=hbs: e.transpose(out=pT[hbs][:, k, :], in_=hb[hbs][:, k * 128:(k + 1) * 128],
                                                                    identity=ident[:])
                            if k < 7:
                                sy.pe_nosig(f, [B_hb[hbs], B_const], [B_pT[hbs]])
                            else:
                                sy.op("pe", f, [B_hb[hbs], B_const], [B_pT[hbs]])
                        ee = "dve" if (tt % 2 == 0) else "act"
                        if ee == "dve":
                            sy.op("dve", lambda e, hbs=hbs, hs=hs, j=j: e.tensor_copy(out=hT[hs][:, :, j * 128:(j + 1) * 128], in_=pT[hbs][:]),
                                  [B_pT[hbs]], [B_hT[hs]])
                        else:
                            sy.op("act", lambda e, hbs=hbs, hs=hs, j=j: e.copy(out=hT[hs][:, :, j * 128:(j + 1) * 128], in_=pT[hbs][:]),
                                  [B_pT[hbs]], [B_hT[hs]])
                    for c4 in range(7):
                        g = (st * 7 + c4) % 2
                        nrows_tot = 0
                        for i in range(4):
                            c = c4 * 4 + i
                            if c * 128 >= NQK:
                                break
                            ncol = min(128, NQK - c * 128)
                            nrows_tot += ncol
                            p = pmi % 4
                            pmi += 1
                            for k in range(8):
                                f = lambda e, k=k, c=c, ncol=ncol, p=p: e.matmul(pm[p][:ncol, :], lhsT=W[:, k, c * 128:c * 128 + ncol],
                                                                                 rhs=hT[hs][:, k, :], start=(k == 0), stop=(k == 7))
                                if k < 7:
                                    sy.pe_nosig(f, [B_W, B_hT[hs]], [B_pm[p]])
                                else:
                                    sy.op("pe", f, [B_W, B_hT[hs]], [B_pm[p]])
                            evi += 1
                            if evi % 2 == 0:
                                sy.op("dve", lambda e, g=g, i=i, ncol=ncol, p=p: e.tensor_copy(out=stgT[g][:ncol, i, :], in_=pm[p][:ncol, :]),
                                      [B_pm[p]], [B_stgT[g]])
                            else:
                                sy.op("act", lambda e, g=g, i=i, ncol=ncol, p=p: e.copy(out=stgT[g][:ncol, i, :], in_=pm[p][:ncol, :]),
                                      [B_pm[p]], [B_stgT[g]])
                        r0 = c4 * 512
                        nfull = nrows_tot // 128
                        if nfull > 0:
                            sy.dma("sp", qkT[r0:r0 + nfull * 128, st * 512:(st + 1) * 512].rearrange("(i p) t -> p i t", p=128),
                                   stgT[g][:, 0:nfull, :], B_stgT[g], False)
                        rem = nrows_tot - nfull * 128
                        if rem > 0:
                            sy.dma("sp", qkT[r0 + nfull * 128:r0 + nfull * 128 + rem, st * 512:(st + 1) * 512],
                                   stgT[g][:rem, nfull, :], B_stgT[g], False)
                    for j in range(4):
                        tt = st * 4 + j
                        g = tt % 2
                        for n in range(7):
                            ncol = min(512, NVZ - n * 512)
                            p = pmi % 4
                            pmi += 1
                            for k in range(8):
                                f = lambda e, k=k, n=n, ncol=ncol, p=p, j=j: e.matmul(
                                    pm[p][:, :ncol], lhsT=hT[hs][:, k, j * 128:(j + 1) * 128],
                                    rhs=W[:, k, NQK + n * 512:NQK + n * 512 + ncol], start=(k == 0), stop=(k == 7))
                                if k < 7:
                                    sy.pe_nosig(f, [B_W, B_hT[hs]], [B_pm[p]])
                                else:
                                    sy.op("pe", f, [B_W, B_hT[hs]], [B_pm[p]])
                            evi += 1
                            if evi % 2 == 0:
                                sy.op("dve", lambda e, g=g, n=n, ncol=ncol, p=p: e.tensor_copy(out=stgV[g][:, n * 512:n * 512 + ncol], in_=pm[p][:, :ncol]),
                                      [B_pm[p]], [B_stgV[g]])
                            else:
                                sy.op("act", lambda e, g=g, n=n, ncol=ncol, p=p: e.copy(out=stgV[g][:, n * 512:n * 512 + ncol], in_=pm[p][:, :ncol]),
                                      [B_pm[p]], [B_stgV[g]])
                        sy.dma("sp", vz[tt * 128:(tt + 1) * 128, :], stgV[g][:], B_stgV[g], False)
                sy.barrier()
            if stop < 2:
                continue

            with contextlib.ExitStack() as ph:
                EB = sb(ph, "EB", [128, 12, 1024], BF16)
                B_EB = Buf("EB")
                aqT = sb(ph, "aqT", [128, 6, 2048], BF16)
                akT = sb(ph, "akT", [128, 3, 2176], BF16)
                bqT = sb(ph, "bqT", [128, 6, 2048], BF16)
                bkT = sb(ph, "bkT", [128, 6, 2, 2048], BF16)
                B_aqT, B_akT, B_bqT = Buf("aqT"), Buf("akT"), Buf("bqT")
                B_bkT = [Buf("bkT0"), Buf("bkT1")]
                NV = 4
                vaug = [sb(ph, "vaug%d" % i, [128, 12, 65], BF16) for i in range(NV)]
                B_vaug = [Buf("vaug%d" % i) for i in range(NV)]
                Et = [sb(ph, "Et%d" % i, [128, 1024], BF16) for i in range(3)]
                B_Et = [Buf("Et%d" % i) for i in range(3)]
                PTt = [sb(ph, "PTt%d" % i, [128, 1024], BF16) for i in range(4)]
                B_PTt = [Buf("PTt%d" % i) for i in range(4)]
                ostg = [sb(ph, "ostg%d" % i, [128, 780], F32) for i in range(2)]
                B_ostg = [Buf("ostg%d" % i) for i in range(2)]
                pS = [ps(ph, "pS%d" % i, [128, 1024], F32) for i in range(3)]
                B_pS = [Buf("pS%d" % i) for i in range(3)]
                pacc = [ps(ph, "pacc%d" % i, [128, 2, 512], F32) for i in range(1)]
                B_pacc = [Buf("pacc%d" % i) for i in range(1)]

                sy.dma("sp", EB[:], ebAB, B_EB, True)
                for i in range(NV):
                    sy.op("pool", lambda e, i=i: e.memset(vaug[i][:], 1.0), [], [B_vaug[i]])

                cnt = {"v": 0, "s": 0, "pt": 0, "blk": 0, "e": 0}

                items = []

                def band_block(pi, nh, G, hasprev, qT, B_q, qsel, kfun, B_k, vcol0, rows_cur, rows_prev, orows):
                    nkv = nh // G
                    tiles = [0, 1] if hasprev else [1]
                    st_ = {}
                    ngrp = nh // 4

                    def stageA(hg):
                        if hg == 0:
                            vs = {}
                            for t in tiles:
                                s_ = cnt["v"] % NV
                                cnt["v"] += 1
                                rows = rows_prev if t == 0 else rows_cur
                                sy.dma("sp", vaug[s_][:, :nkv, 0:64],
                                       vz[rows, vcol0:vcol0 + nkv * 64].rearrange("p (h e) -> p h e", e=64), B_vaug[s_], True)
                                vs[t] = s_
                            st_["vs"] = vs
                            st_["ab"] = cnt["blk"] % 2
                            cnt["blk"] += 1
                        ss = cnt["s"] % 3
                        cnt["s"] += 1
                        i = 0
                        for t in (0, 1):
                            tk = t if hasprev else 1
                            for hh in range(4):
                                h = hg * 4 + hh
                                hp = (h % 2) * 64
                                col = (h % 2) * 512 + (t * 2 + hh // 2) * 128
                                f = lambda e, tk=tk, col=col, h=h, hp=hp: e.matmul(
                                    pS[ss][:, col:col + 128], lhsT=kfun(h, tk), rhs=qT[hp:hp + 64, h // 2, qsel], start=True, stop=True)
                                i += 1
                                if i < 8:
                                    sy.pe_nosig(f, [B_q] + B_k, [B_pS[ss]])
                                else:
                                    sy.op("pe", f, [B_q] + B_k, [B_pS[ss]])
                        es = cnt["e"] % 3
                        cnt["e"] += 1
                        pt = cnt["pt"] % 4
                        cnt["pt"] += 1
                        st_[("pt", hg)] = pt
                        sy.op("act", lambda e: e.activation(out=Et[es][:], in_=pS[ss][:], func=AF.Exp, scale=0.125),
                              [B_pS[ss]], [B_Et[es]])
                        sy.op("dve", lambda e: e.tensor_tensor(out=PTt[pt][:], in0=Et[es][:], in1=EB[:, pi * 3 + hg, :], op=ALU.mult),
                              [B_Et[es], B_EB], [B_PTt[pt]])

                    def stageB(hg):
                        vs, ob_, pt = st_["vs"], st_["ab"], st_[("pt", hg)]
                        ab = 0
                        for hh in range(4):
                            h = hg * 4 + hh
                            for ti, t in enumerate(tiles):
                                c0 = (h % 2) * 512 + (t * 2 + hh // 2) * 128
                                f = lambda e, t=t, h=h, c0=c0, ti=ti: e.matmul(
                                    pacc[ab][:, h // 6, (h % 6) * 65:(h % 6) * 65 + 65], lhsT=PTt[pt][:, c0:c0 + 128],
                                    rhs=vaug[vs[t]][:, h // G, :], start=(ti == 0), stop=(ti == len(tiles) - 1))
                                last = (hh == 3 and ti == len(tiles) - 1)
                                rb = [B_PTt[pt]] + [B_vaug[vs[x]] for x in tiles]
                                if not last:
                                    sy.pe_nosig(f, rb, [B_pacc[ab]])
                                else:
                                    sy.op("pe", f, rb, [B_pacc[ab]])
                        if hg != ngrp - 1:
                            return
                        nb = nh * 65
                        n0 = min(nb, 390)
                        sy.op("act", lambda e: e.copy(out=ostg[ob_][:, 0:n0], in_=pacc[ab][:, 0, 0:n0]), [B_pacc[ab]], [B_ostg[ob_]])
                        if nb > 390:
                            sy.op("dve", lambda e: e.tensor_copy(out=ostg[ob_][:, 390:nb], in_=pacc[ab][:, 1, 0:nb - 390]),
                                  [B_pacc[ab]], [B_ostg[ob_]])
                        sy.dma("sp", OB[pi, orows, 0:nb], ostg[ob_][:, 0:nb], B_ostg[ob_], False)

                    for hg in range(ngrp):
                        items.append(("step", (lambda hg=hg: stageA(hg)), (lambda hg=hg: stageB(hg))))

                for sp in range(NSP):
                    t0 = sp * 2048
                    slot = sp % 2

                    def span_loads(sp=sp, t0=t0, slot=slot):
                        sy.dma("sp", aqT[:], qkT[AQ0:AQ0 + 768, t0:t0 + 2048].rearrange("(c p) t -> p c t", p=128), B_aqT, True)
                        sy.dma("sp", bqT[:], qkT[BQ0:BQ0 + 768, t0:t0 + 2048].rearrange("(c p) t -> p c t", p=128), B_bqT, True)
                        sy.dma("sp", bkT[:, :, slot, :], qkT[BK0:BK0 + 768, t0:t0 + 2048].rearrange("(c p) t -> p c t", p=128),
                               B_bkT[slot], True)
                        a0 = 0 if sp > 0 else 128
                        for half in range(2):
                            sy.dma("sp", akT[half * 64:(half + 1) * 64, :, a0:2176],
                                   qkT[AK0:AK0 + 192, t0 - 128 + a0:t0 + 2048].rearrange("(g p) t -> p g t", p=64), B_akT, True)
                    items.append(("load", span_loads))
                    for i in range(16):
                        hasprev = (sp > 0 or i > 0)
                        qsel = slice(128 * i, 128 * i + 128)

                        def kfunA(h, t, i=i):
                            hp = (h % 2) * 64
                            c = 128 * i + 128 * t
                            return akT[hp:hp + 64, h // 4, c:c + 128]
                        rc = slice(t0 + 128 * i, t0 + 128 * i + 128)
                        rp = slice(t0 + 128 * i - 128, t0 + 128 * i)
                        band_block(0, 12, 4, hasprev, aqT, B_aqT, qsel, kfunA, [B_akT], AV0, rc, rp, rc)
                    for i in range(16):
                        hasprev = (sp > 0 or i > 0)
                        qsel = slice(128 * i, 128 * i + 128)

                        def kfunB1(h, t, i=i, slot=slot):
                            hp = (h % 2) * 64
                            if t == 1:
                                return bkT[hp:hp + 64, h // 2, slot, 128 * i:128 * i + 128]
                            if i > 0:
                                return bkT[hp:hp + 64, h // 2, slot, 128 * i - 128:128 * i]
                            return bkT[hp:hp + 64, h // 2, 1 - slot, 1920:2048]
                        rc = slice(t0 + 128 * i, t0 + 128 * i + 128)
                        rp = slice(t0 + 128 * i - 128, t0 + 128 * i)
                        band_block(1, 12, 1, hasprev, bqT, B_bqT, qsel, kfunB1, B_bkT, BV0, rc, rp, rc)
                    for n4 in range(4):
                        for r in range(4):
                            hasprev = (sp > 0 or n4 > 0)
                            qsel = slice(512 * n4 + r, 512 * n4 + 512, 4)

                            def kfunB4(h, t, n4=n4, r=r, slot=slot):
                                hp = (h % 2) * 64
                                if t == 1:
                                    return bkT[hp:hp + 64, h // 2, slot, 512 * n4 + r:512 * n4 + 512:4]
                                if n4 > 0:
                                    return bkT[hp:hp + 64, h // 2, slot, 512 * (n4 - 1) + r:512 * n4:4]
                                return bkT[hp:hp + 64, h // 2, 1 - slot, 1536 + r:2048:4]
                            b0 = t0 + 512 * n4 + r
                            rc = slice(b0, b0 + 509, 4)
                            rp = slice(b0 - 512, b0 - 3, 4)
                            band_block(2, 12, 1, hasprev, bqT, B_bqT, qsel, kfunB4, B_bkT, BV0, rc, rp, rc)
                    for r in range(16):
                        hasprev = (sp > 0)
                        qsel = slice(r, 2048, 16)

                        def kfunB16(h, t, r=r, slot=slot):
                            hp = (h % 2) * 64
                            sl = slot if t == 1 else 1 - slot
                            return bkT[hp:hp + 64, h // 2, sl, r:2048:16]
                        b0 = t0 + r
                        rc = slice(b0, b0 + 2033, 16)
                        rp = slice(b0 - 2048, b0 - 15, 16)
                        band_block(3, 12, 1, hasprev, bqT, B_bqT, qsel, kfunB16, B_bkT, BV0, rc, rp, rc)
                pend = []
                LOOK2 = 2
                for it in items:
                    if it[0] == "load":
                        it[1]()
                    else:
                        it[1]()
                        pend.append(it[2])
                        if len(pend) > LOOK2:
                            pend.pop(0)()
                while pend:
                    pend.pop(0)()
                sy.barrier()
            if stop < 3:
                continue

            with contextlib.ExitStack() as ph:
                strip = sb(ph, "strip", [128, 4, CSTRIP], BF16)
                B_strip = Buf("strip")
                QT = [sb(ph, "cQT%d" % i, [128, S], BF16) for i in range(2)]
                K1 = [sb(ph, "cK1%d" % i, [128, S], BF16) for i in range(2)]
                K2 = [sb(ph, "cK2%d" % i, [128, S], BF16) for i in range(2)]
                VA = [sb(ph, "cVA%d" % i, [128, NT, 129], BF16) for i in range(2)]
                B_QT = [Buf("cQT%d" % i) for i in range(2)]
                B_K1 = [Buf("cK1%d" % i) for i in range(2)]
                B_K2 = [Buf("cK2%d" % i) for i in range(2)]
                B_VA = [Buf("cVA%d" % i) for i in range(2)]
                NE, NP = 4, 4
                Ec = [sb(ph, "Ec%d" % i, [128, 512], BF16) for i in range(NE)]
                B_Ec = [Buf("Ec%d" % i) for i in range(NE)]
                Pc = [sb(ph, "Pc%d" % i, [128, 512], BF16) for i in range(NP)]
                B_Pc = [Buf("Pc%d" % i) for i in range(NP)]
                Y1 = sb(ph, "Y1", [128, 4, 128], F32)
                B_Y1 = [Buf("Y1_%d" % i) for i in range(4)]
                yd = [sb(ph, "yd%d" % i, [128, 128], F32) for i in range(2)]
                B_yd = [Buf("yd%d" % i) for i in range(2)]
                sm = [sb(ph, "sm%d" % i, [128, 8], F32) for i in range(2)]
                B_sm = [Buf("sm%d" % i) for i in range(2)]
                cjunk = sb(ph, "cjunk", [128, 128], F32)
                B_cjunk = Buf("cjunk")
                ystg = [sb(ph, "ystg%d" % i, [128, 4, 128], F32) for i in range(2)]
                B_ystg = [Buf("ystg%d" % i) for i in range(2)]
                NSB = 4
                pSc = [ps(ph, "pSc%d" % i, [128, 512], F32) for i in range(NSB)]
                B_pSc = [Buf("pSc%d" % i) for i in range(NSB)]
                pA = [ps(ph, "pA%d" % i, [128, 512], F32) for i in range(4)]
                B_pA = [Buf("pA%d" % i) for i in range(4)]

                sy.dma("sp", strip[:], ebC, B_strip, True)
                for i in range(2):
                    sy.op("pool", lambda e, i=i: e.memset(K1[i][64:128, :], 0.0), [], [B_K1[i]])
                    sy.op("pool", lambda e, i=i: e.memset(K2[i][0:64, :], 0.0), [], [B_K2[i]])
                    sy.op("pool", lambda e, i=i: e.memset(VA[i][:, :, 128:129], 1.0), [], [B_VA[i]])

                def load_head(hc):
                    b = hc % 2
                    sy.dma("sp", QT[b][:], qkT[CQ0 + hc * 128:CQ0 + (hc + 1) * 128, :], B_QT[b], True)
                    sy.dma("sp", K1[b][0:64, :], qkT[CK0 + hc * 128:CK0 + hc * 128 + 64, :], B_K1[b], True)
                    sy.dma("sp", K2[b][64:128, :], qkT[CK0 + hc * 128 + 64:CK0 + (hc + 1) * 128, :], B_K2[b], True)
                    nchunk = max(1, NT // 16)
                    tpc = NT // nchunk
                    for ci in range(nchunk):
                        sy.dma("sp", VA[b][:, ci * tpc:(ci + 1) * tpc, 0:128],
                               vz[ci * tpc * 128:(ci + 1) * tpc * 128, CV0 + hc * 128:CV0 + (hc + 1) * 128].rearrange("(t p) e -> p t e", p=128),
                               B_VA[b], True)

                accs = [sb(ph, "accs%d" % i, [128, 4, 129], F32) for i in range(2)]
                B_accs = [Buf("accs%d" % i) for i in range(2)]
                mhalf = sb(ph, "mhalf", [128, 1], F32)
                sy.op("pool", lambda e: e.memset(mhalf[:], -0.5), [], [B_const])

                load_head(0)
                steps = []
                for hc in range(4):
                    for qc in range(NST):
                        for m in range(2):
                            for kt in range(4 * qc + 4):
                                steps.append((hc, qc, m, kt))
                LOOK = 3
                cix = {"e": 0, "g": 0}
                loaded = {0}

                def stageA(i):
                    hc, qc, m, kt = steps[i]
                    b = hc % 2
                    KP, B_KP = (K1[b], B_K1[b]) if m == 0 else (K2[b], B_K2[b])
                    q0 = max(qc * 512, kt * 128)
                    nq = (qc + 1) * 512 - q0
                    dmin = q0 // 128 - kt
                    s_ = i % NSB
                    p = i % NP
                    sy.op("pe", lambda e: e.matmul(pSc[s_][:, :nq], lhsT=KP[:, kt * 128:(kt + 1) * 128], rhs=QT[b][:, q0:q0 + nq],
                                                   start=True, stop=True), [B_KP, B_QT[b]], [B_pSc[s_]])
                    if dmin >= CNEAR:
                        sy.op("act", lambda e: e.activation(out=Pc[p][:, :nq], in_=pSc[s_][:, :nq], func=AF.Exp, scale=0.125),
                              [B_pSc[s_]], [B_Pc[p]])
                    else:
                        ei = cix["e"] % NE
                        cix["e"] += 1
                        x0 = q0 - kt * 128 + 384
                        sy.op("act", lambda e: e.activation(out=Ec[ei][:, :nq], in_=pSc[s_][:, :nq], func=AF.Exp, scale=0.125),
                              [B_pSc[s_]], [B_Ec[ei]])
                        sy.op("dve", lambda e: e.tensor_tensor(out=Pc[p][:, :nq], in0=Ec[ei][:, :nq], in1=strip[:, hc, x0:x0 + nq],
                                                               op=ALU.mult), [B_Ec[ei], B_strip], [B_Pc[p]])

                def stageB(i):
                    hc, qc, m, kt = steps[i]
                    b = hc % 2
                    if (hc + 1) not in loaded and hc + 1 < 4:
                        loaded.add(hc + 1)
                        load_head(hc + 1)
                    q0 = max(qc * 512, kt * 128)
                    p = i % NP
                    qts = [qt for qt in range(4 * qc, 4 * qc + 4) if qt >= kt]
                    for qi, qt in enumerate(qts):
                        jj = qt - q0 // 128
                        a = qt % 4
                        f = lambda e, a=a, jj=jj, qt=qt: e.matmul(pA[a][:, 0:129], lhsT=Pc[p][:, jj * 128:(jj + 1) * 128],
                                                                  rhs=VA[b][:, kt, :], start=(kt == 0), stop=(kt == qt))
                        if qi < len(qts) - 1:
                            sy.pe_nosig(f, [B_Pc[p], B_VA[b]], [B_pA[a]])
                        else:
                            wl = [B_pA[x % 4] for x in qts]
                            sy.op("pe", f, [B_Pc[p], B_VA[b]], wl)
                    if kt != 4 * qc + 3:
                        return
                    g = cix["g"] % 2
                    cix["g"] += 1
                    AC, B_AC = accs[g], B_accs[g]
                    yb = (hc * NST + qc) % 2
                    for a in range(4):
                        sy.op("dve", lambda e, a=a: e.tensor_copy(out=AC[:, a, :], in_=pA[a][:, 0:129]), [B_pA[a]], [B_AC])
                    for a in range(4):
                        smb, B_smb = sm[a % 2], B_sm[a % 2]
                        sy.op("dve", lambda e, a=a, smb=smb: e.reciprocal(out=smb[:, 0:1], in_=AC[:, a, 128:129]), [B_AC], [B_smb])
                        if m == 0:
                            sy.op("dve", lambda e, a=a, smb=smb: e.tensor_scalar(out=Y1[:, a, :], in0=AC[:, a, 0:128], scalar1=smb[:, 0:1],
                                                                                 scalar2=None, op0=ALU.mult), [B_AC, B_smb], [B_Y1[a]])
                        else:
                            ydb, B_ydb = yd[a % 2], B_yd[a % 2]
                            sy.op("dve", lambda e, smb=smb: e.tensor_tensor(out=smb[:, 1:2], in0=smb[:, 0:1], in1=neglam[:, l:l + 1],
                                                                            op=ALU.mult), [B_smb, B_const], [B_smb])
                            sy.op("dve", lambda e, a=a, smb=smb, ydb=ydb: e.scalar_tensor_tensor(
                                out=ydb[:], in0=AC[:, a, 0:128], scalar=smb[:, 1:2], in1=Y1[:, a, :], op0=ALU.mult, op1=ALU.add),
                                [B_AC, B_smb, B_Y1[a]], [B_ydb])
                            sy.op("dve", lambda e, smb=smb, ydb=ydb: e.scalar_tensor_tensor(
                                out=cjunk[:], in0=ydb[:], scalar=1.0, in1=ydb[:], op0=ALU.mult, op1=ALU.mult, accum_out=smb[:, 2:3]),
                                [B_ydb], [B_cjunk, B_smb])
                            sy.op("dve", lambda e, smb=smb: e.tensor_scalar(out=smb[:, 3:4], in0=smb[:, 2:3], scalar1=1.0 / 128, scalar2=EPS,
                                                                            op0=ALU.mult, op1=ALU.add), [B_smb], [B_smb])
                            sy.op("pool", lambda e, smb=smb: e.tensor_tensor(out=smb[:, 5:6], in0=smb[:, 3:4], in1=mhalf[:], op=ALU.pow),
                                  [B_smb, B_const], [B_smb])
                            sy.op("dve", lambda e, a=a, smb=smb, ydb=ydb: e.scalar_tensor_tensor(
                                out=ystg[yb][:, a, :], in0=ydb[:], scalar=smb[:, 5:6], in1=gsubb[:, l, :], op0=ALU.mult, op1=ALU.mult),
                                [B_ydb, B_smb, B_const], [B_ystg[yb]])
                    if m == 1:
                        sy.dma("sp", YC[qc * 512:(qc + 1) * 512, hc * 128:(hc + 1) * 128].rearrange("(j p) e -> p j e", p=128),
                               ystg[yb][:], B_ystg[yb], False)

                nsteps = len(steps)
                for i in range(nsteps + LOOK):
                    if i < nsteps:
                        stageA(i)
                    if i >= LOOK:
                        stageB(i - LOOK)
                sy.barrier()
            if stop < 4:
                continue

            with contextlib.ExitStack() as ph:
                WO = sb(ph, "WO", [128, 16, D], BF16)
                B_WO = Buf("WO")
                gt = sb(ph, "p4gt", [128, D], F32)
                B_gt = Buf("p4gt")
                mh4 = sb(ph, "mh4", [128, 1], F32)
                B_mh4 = Buf("mh4")

                def dbl(name, shape, dt, n=2):
                    return [sb(ph, "%s%d" % (name, i), shape, dt) for i in range(n)], [Buf("%s%d" % (name, i)) for i in range(n)]
                oa, B_oa = dbl("oa", [128, 12, 65], F32)
                ob, B_ob = dbl("ob", [128, 3, 780], F32)
                yct, B_yct = dbl("yct", [128, 512], F32)
                zt, B_zt = dbl("zt", [128, 2048], BF16)
                xr, B_xr = dbl("xr", [128, D], F32)
                obs, B_obs = dbl("obs", [128, 12, 65], F32)
                rr, B_rr = dbl("rr", [128, 2, 12], F32)
                yy, B_yy = dbl("yy", [128, 2048], F32)
                szt, B_szt = dbl("szt", [128, 2048], F32)
                yg, B_yg = dbl("yg", [128, 2048], BF16)
                ygT, B_ygT = dbl("ygT", [128, 16, 128], BF16)
                fs, B_fs = dbl("fs", [128, 4], F32)
                tn, B_tn = dbl("tn", [128, D], F32)
                og, B_og = dbl("og", [128, D], F32)
                fj = sb(ph, "fj", [128, D], BF16)
                B_fj = Buf("fj")
                pTy = [ps(ph, "pTy%d" % i, [128, 16, 128], BF16) for i in range(2)]
                B_pTy = [Buf("pTy%d" % i) for i in range(2)]
                po = [ps(ph, "po%d" % i, [128, 1024], F32) for i in range(2)]
                B_po = [Buf("po%d" % i) for i in range(2)]

                for k in range(16):
                    sy.dma("sp", WO[:, k, :], wbf_out[l, k * 128:(k + 1) * 128, :], B_WO, True)
                sy.dma("sp", gt[:], modb[l, 2], B_gt, True)
                sy.op("pool", lambda e: e.memset(mh4[:], -0.5), [], [B_mh4])

                def L1(tt):
                    s = tt % 2
                    rows = slice(tt * 128, (tt + 1) * 128)
                    sy.dma("sp", oa[s][:].rearrange("p h e -> p (h e)"), OB[0, rows, :], B_oa[s], True)
                    sy.dma("sp", ob[s][:], OB[1:4, rows, :].rearrange("c p f -> p c f"), B_ob[s], True)
                    sy.dma("sp", yct[s][:], YC[rows, :], B_yct[s], True)
                    sy.dma("sp", zt[s][:], vz[rows, Z0:Z0 + 2048], B_zt[s], True)

                def L3(tt):
                    s = tt % 2
                    sy.dma("sp", xr[s][:], x_src[tt * 128:(tt + 1) * 128, :], B_xr[s], True)

                def S1(tt):
                    s = tt % 2
                    obv = lambda c: ob[s][:, c, :].rearrange("p (h e) -> p h e", e=65)
                    sy.op("act", lambda e: e.activation(out=szt[s][:], in_=zt[s][:], func=AF.Silu), [B_zt[s]], [B_szt[s]])
                    sy.op("pool", lambda e: e.tensor_tensor(out=obs[s][:], in0=obv(0), in1=obv(1), op=ALU.add), [B_ob[s]], [B_obs[s]])
                    sy.op("pool", lambda e: e.tensor_tensor(out=obs[s][:], in0=obs[s][:], in1=obv(2), op=ALU.add), [B_ob[s], B_obs[s]], [B_obs[s]])
                    sy.op("dve", lambda e: e.tensor_tensor(out=rr[s][:, 0, :], in0=oa[s][:, :, 64], in1=esink[:, l, :], op=ALU.add),
                          [B_oa[s], B_const], [B_rr[s]])
                    sy.op("dve", lambda e: e.reciprocal(out=rr[s][:, 0, :], in_=rr[s][:, 0, :]), [B_rr[s]], [B_rr[s]])
                    sy.op("dve", lambda e: e.tensor_tensor(out=yy[s][:, 0:768].rearrange("p (h e) -> p h e", e=64), in0=oa[s][:, :, 0:64],
                                                           in1=rr[s][:, 0, :].unsqueeze(2).to_broadcast([128, 12, 64]), op=ALU.mult),
                          [B_oa[s], B_rr[s]], [B_yy[s]])
                    sy.op("dve", lambda e: e.tensor_tensor(out=yg[s][:, 0:768], in0=yy[s][:, 0:768], in1=szt[s][:, 0:768], op=ALU.mult),
                          [B_yy[s], B_szt[s]], [B_yg[s]])
                    sy.op("dve", lambda e: e.reciprocal(out=rr[s][:, 1, :], in_=obs[s][:, :, 64]), [B_obs[s]], [B_rr[s]])
                    sy.op("pool", lambda e: e.tensor_tensor(out=yy[s][:, 768:1536].rearrange("p (h e) -> p h e", e=64), in0=obs[s][:, :, 0:64],
                                                            in1=rr[s][:, 1, :].unsqueeze(2).to_broadcast([128, 12, 64]), op=ALU.mult),
                          [B_obs[s], B_rr[s]], [B_yy[s]])
                    sy.op("dve", lambda e: e.tensor_tensor(out=yg[s][:, 768:1536], in0=yy[s][:, 768:1536], in1=szt[s][:, 768:1536], op=ALU.mult),
                          [B_yy[s], B_szt[s]], [B_yg[s]])
                    sy.op("pool", lambda e: e.tensor_tensor(out=yg[s][:, 1536:2048], in0=yct[s][:], in1=szt[s][:, 1536:2048], op=ALU.mult),
                          [B_yct[s], B_szt[s]], [B_yg[s]])

                def T2(tt):
                    s = tt % 2
                    for k in range(16):
                        f = lambda e, k=k: e.transpose(out=pTy[s][:, k, :], in_=yg[s][:, k * 128:(k + 1) * 128], identity=ident[:])
                        if k % 8 < 7:
                            sy.pe_nosig(f, [B_yg[s], B_const], [B_pTy[s]])
                        else:
                            sy.op("pe", f, [B_yg[s], B_const], [B_pTy[s]])
                    sy.op("act", lambda e: e.copy(out=ygT[s][:, 0:8, :], in_=pTy[s][:, 0:8, :]), [B_pTy[s]], [B_ygT[s]])
                    sy.op("act", lambda e: e.copy(out=ygT[s][:, 8:16, :], in_=pTy[s][:, 8:16, :]), [B_pTy[s]], [B_ygT[s]])

                def M2(tt):
                    s = tt % 2
                    for n in range(2):
                        for k in range(16):
                            f = lambda e, k=k, n=n: e.matmul(po[s][:, n * 512:(n + 1) * 512], lhsT=ygT[s][:, k, :],
                                                             rhs=WO[:, k, n * 512:(n + 1) * 512], start=(k == 0), stop=(k == 15))
                            if k < 15 or n == 0:
                                sy.pe_nosig(f, [B_ygT[s], B_WO], [B_po[s]])
                            else:
                                sy.op("pe", f, [B_ygT[s], B_WO], [B_po[s]])

                def S3(tt):
                    s = tt % 2
                    f4, B4 = fs[s], B_fs[s]
                    sy.op("act", lambda e: e.activation(out=fj[:], in_=po[s][:], func=AF.Square, accum_out=f4[:, 0:1]), [B_po[s]], [B_fj, B4])
                    sy.op("dve", lambda e: e.tensor_scalar(out=f4[:, 1:2], in0=f4[:, 0:1], scalar1=1.0 / D, scalar2=EPS, op0=ALU.mult, op1=ALU.add),
                          [B4], [B4])
                    sy.op("pool", lambda e: e.tensor_tensor(out=f4[:, 3:4], in0=f4[:, 1:2], in1=mh4[:], op=ALU.pow), [B4, B_mh4], [B4])
                    sy.op("dve", lambda e: e.scalar_tensor_tensor(out=tn[s][:], in0=po[s][:], scalar=f4[:, 3:4], in1=gt[:], op0=ALU.mult, op1=ALU.mult),
                          [B_po[s], B4, B_gt], [B_tn[s]])
                    sy.op("pool", lambda e: e.tensor_tensor(out=og[s][:], in0=tn[s][:], in1=xr[s][:], op=ALU.add), [B_tn[s], B_xr[s]], [B_og[s]])
                    sy.dma("sp", x_dst[tt * 128:(tt + 1) * 128, :], og[s][:], B_og[s], False)

                ok = lambda t: 0 <= t < NT
                for i in range(-3, NT + 1):
                    if ok(i + 3):
                        L1(i + 3)
                    if ok(i):
                        L3(i)
                    if ok(i + 2):
                        S1(i + 2)
                    if ok(i + 1):
                        T2(i + 1)
                    if ok(i):
                        M2(i)
                    if ok(i - 1):
                        S3(i - 1)
                sy.barrier()
        print("build done: sems=%d counts=%s" % (sy.nsem, sy.cnt))
    return nc


def _rel_bucket(dist):
    n = np.maximum(dist, 0).astype(np.int64)
    nf = np.maximum(n, 1).astype(np.float32)
    large = 16 + (np.log(nf / np.float32(16.0)) / np.float32(math.log(2048 / 16)) * np.float32(16.0)).astype(np.int32)
    large = np.minimum(large, 31)
    return np.where(n < 16, n, large).astype(np.int64)


def _host_tables(rel_table):
    rel_table = np.asarray(rel_table, dtype=np.float32)
    k = np.arange(128)[:, None]
    q = np.arange(128)[None, :]
    biasAB = np.zeros((128, 12, 2, 2, 2, 128), np.float32)
    maskAB = np.zeros((128, 12, 2, 2, 2, 128), np.float32)
    pats = [(1, 127, 0), (1, 128, 12), (4, 128, 12), (16, 128, 12)]
    for pi, (d, maxd, ho) in enumerate(pats):
        for t in range(2):
            dist = q - k + (128 if t == 0 else 0)
            idx = _rel_bucket(dist * d)
            msk = ((dist >= 0) & (dist <= maxd)).astype(np.float32)
            for hg in range(3):
                for hh in range(4):
                    biasAB[:, pi * 3 + hg, hh % 2, t, hh // 2, :] = rel_table[idx, ho + hg * 4 + hh]
                    maskAB[:, pi * 3 + hg, hh % 2, t, hh // 2, :] = msk
    biasAB = biasAB.reshape(128, 12, 1024)
    maskAB = maskAB.reshape(128, 12, 1024)
    xx = np.arange(CSTRIP)[None, :]
    dist = xx - 384 - k
    idx = _rel_bucket(dist)
    biasC = np.stack([rel_table[idx, 24 + h] for h in range(4)], axis=1).astype(np.float32)
    maskC = (dist >= 0).astype(np.float32)
    b31 = rel_table[31, 24:28].reshape(1, 4).astype(np.float32)
    return biasAB, maskAB, np.ascontiguousarray(biasC), maskC, b31


def _perm_cols():
    sizes = [768, 192, 192, 768, 768, 768, 512, 512, 512, 2048]
    offs = np.cumsum([0] + sizes)
    seg = lambda i: np.arange(offs[i], offs[i + 1])
    order = [0, 1, 3, 4, 6, 7, 2, 5, 8, 9]
    return np.concatenate([seg(i) for i in order])


_NC_CACHE = {}
ACTIVE = {0: 0, 1: 1, 4: 2, 5: 3}


def make_in_maps(inputs, S=8192):
    x = np.asarray(inputs["x"], np.float32)
    c = np.asarray(inputs["c"], np.float32)
    perm = _perm_cols()
    w_in_p = np.ascontiguousarray(np.asarray(inputs["w_in"], np.float32)[:, :, perm])
    biasAB, maskAB, biasC, maskC, b31 = _host_tables(inputs["rel_table"])
    lam4 = np.stack([np.asarray(inputs[k], np.float32) for k in ("lam_q1", "lam_k1", "lam_q2", "lam_k2")], axis=1).reshape(NL, 256)
    shared = {
        "w_in": w_in_p,
        "w_out": np.ascontiguousarray(np.asarray(inputs["w_out"], np.float32)),
        "w_ada": np.ascontiguousarray(np.asarray(inputs["w_ada"], np.float32)),
        "b_ada": np.ascontiguousarray(np.asarray(inputs["b_ada"], np.float32)),
        "g_pre": np.ascontiguousarray(np.asarray(inputs["g_pre"], np.float32)),
        "g_post": np.ascontiguousarray(np.asarray(inputs["g_post"], np.float32)),
        "a_sinks": np.ascontiguousarray(np.asarray(inputs["a_sinks"], np.float32)),
        "lam4": np.ascontiguousarray(lam4),
        "g_sub": np.ascontiguousarray(np.asarray(inputs["g_sub"], np.float32)),
        "biasAB": biasAB, "maskAB": maskAB, "biasC": biasC, "maskC": maskC, "b31": b31,
        "ident": np.eye(128, dtype=np.float32),
    }
    in_maps = []
    zero_shared = None
    for core in range(8):
        b = ACTIVE.get(core)
        if b is None:
            if zero_shared is None:
                zero_shared = {k: (v if k in ("ident", "maskAB", "maskC") else np.zeros_like(v)) for k, v in shared.items()}
                zero_shared["xb"] = np.zeros((S, D), np.float32)
                zero_shared["c2"] = np.zeros((128, 8), np.float32)
            in_maps.append(zero_shared)
            continue
        m = dict(shared)
        m["xb"] = np.ascontiguousarray(x[b, :S])
        m["c2"] = np.ascontiguousarray(c[b].reshape(8, 128).T)
        in_maps.append(m)
    return in_maps


def kernel(x, c, rel_table, w_in, w_out, w_ada, b_ada, g_pre, g_post, a_sinks,
           lam_q1, lam_k1, lam_q2, lam_k2, g_sub):
    inputs = dict(x=x, c=c, rel_table=rel_table, w_in=w_in, w_out=w_out, w_ada=w_ada, b_ada=b_ada, g_pre=g_pre,
                  g_post=g_post, a_sinks=a_sinks, lam_q1=lam_q1, lam_k1=lam_k1, lam_q2=lam_q2, lam_k2=lam_k2, g_sub=g_sub)
    if "nc" not in _NC_CACHE:
        _NC_CACHE["nc"] = build()
    nc = _NC_CACHE["nc"]
    in_maps = make_in_maps(inputs)
    res = run_bass_kernel_spmd(nc, in_maps, core_ids=list(range(8)))
    core_of = {b: core for core, b in ACTIVE.items()}
    outs = [np.asarray(res.results[core_of[b]]["out"], np.float32) for b in range(4)]
    return np.stack(outs, axis=0)
```
